# Optimizing a Trainium2 kernel written in Bass

```python
import jax, jax.numpy as jnp
from jax import lax
import numpy as np

D_MODEL = 1024
BATCH = 8
SEQ = 2048
DEPTH = 2

CHUNK = 64
D_MIX = D_MODEL
HEAD_DIM = 64
SB_HEADS = D_MIX // 2 // HEAD_DIM
SB_WIDTH = SB_HEADS * HEAD_DIM
SC_GROUPS = D_MIX // 4 // HEAD_DIM
SC_WIDTH = SC_GROUPS * HEAD_DIM
SG_HEADS = D_MIX // 4 // HEAD_DIM
SG_WIDTH = SG_HEADS * HEAD_DIM
SB_BLOCK = 128
SC_KERNEL = 3
SG_CHUNK = 128
D_FF = 4 * D_MODEL
PROJ_WIDTH = 3 * SB_WIDTH + 3 * SC_WIDTH + 2 * SG_WIDTH
N_MOD = 6
EPS = 1e-6

kernel_name = "hybrid_sb_conv_gmlp_adaln_trunk"


def rmsnorm(x, g):
    x32 = x.astype(jnp.float32)
    y = x32 * lax.rsqrt(jnp.mean(x32 * x32, axis=-1, keepdims=True) + EPS)
    return (y * g.astype(jnp.float32)).astype(x.dtype)


def stick_breaking_attention(q, k, v):
    bsz, seq, nh, hd = q.shape
    scale = hd ** -0.5
    q32 = q.astype(jnp.float32)
    k32 = k.astype(jnp.float32)
    v32 = v.astype(jnp.float32)
    outs = []
    for i in range(seq // SB_BLOCK):
        kv_len = (i + 1) * SB_BLOCK
        q_blk = q32[:, i * SB_BLOCK:(i + 1) * SB_BLOCK]
        k_ctx = k32[:, :kv_len]
        v_ctx = v32[:, :kv_len]
        z = jnp.einsum('bqhd,bkhd->bhqk', q_blk, k_ctx) * scale
        t_pos = i * SB_BLOCK + jnp.arange(SB_BLOCK)
        s_pos = jnp.arange(kv_len)
        mask = s_pos[None, :] < t_pos[:, None]
        log_beta = jax.nn.log_sigmoid(z)
        log_stay = jnp.where(mask, jax.nn.log_sigmoid(-z), 0.0)
        log_after = lax.cumsum(log_stay, axis=3, reverse=True) - log_stay
        w = jnp.where(mask, jnp.exp(log_beta + log_after), 0.0)
        outs.append(jnp.einsum('bhqk,bkhd->bqhd', w, v_ctx))
    return jnp.concatenate(outs, axis=1).astype(q.dtype)


def short_gated_conv(b_gate, c_gate, h, conv_w, conv_b):
    u = c_gate * h
    y = lax.conv_general_dilated(
        u, conv_w[:, None, :].astype(u.dtype), window_strides=(1,),
        padding=[(SC_KERNEL - 1, 0)], dimension_numbers=('NWC', 'WIO', 'NWC'),
        feature_group_count=SC_WIDTH)
    return b_gate * (y + conv_b)


def spatial_gating(u, v, norm_g, sw, sb):
    bsz, seq, _ = v.shape
    u = jax.nn.gelu(u)
    v = rmsnorm(jax.nn.gelu(v), norm_g)
    n_win = seq // SG_CHUNK
    v = v.reshape(bsz, n_win, SG_CHUNK, SG_HEADS, HEAD_DIM)
    chunk_id = jnp.arange(SG_CHUNK) // CHUNK
    mask = chunk_id[:, None] >= chunk_id[None, :]
    w = jnp.where(mask[None], sw, jnp.zeros_like(sw))
    mixed = jnp.einsum('hts,bnshd->bnthd', w, v) + sb.T[None, None, :, :, None]
    return u * mixed.reshape(bsz, seq, SG_WIDTH)


def setup_inputs(seed: int = 0) -> dict:
    key = jax.random.key(seed)
    ks = jax.random.split(key, 16)
    f32 = jnp.float32
    L = DEPTH
    x = jax.random.normal(ks[0], (BATCH, SEQ, D_MODEL), f32)
    c = jax.random.normal(ks[1], (BATCH, D_MODEL), f32)
    ada_w = jax.random.normal(ks[2], (L, D_MODEL, N_MOD * D_MODEL), f32) * D_MODEL ** -0.5
    ada_b = jax.random.normal(ks[3], (L, N_MOD * D_MODEL), f32) * 0.02
    norm_mix_g = 1.0 + 0.02 * jax.random.normal(ks[4], (L, D_MODEL), f32)
    norm_mlp_g = 1.0 + 0.02 * jax.random.normal(ks[5], (L, D_MODEL), f32)
    w_in = jax.random.normal(ks[6], (L, D_MODEL, PROJ_WIDTH), f32) * D_MODEL ** -0.5
    conv_w = jax.random.normal(ks[7], (L, SC_KERNEL, SC_WIDTH), f32) * SC_KERNEL ** -0.5
    conv_b = jax.random.normal(ks[8], (L, SC_WIDTH), f32) * 0.02
    gmlp_norm_g = 1.0 + 0.02 * jax.random.normal(ks[9], (L, SG_WIDTH), f32)
    spatial_w = jax.random.normal(ks[10], (L, SG_HEADS, SG_CHUNK, SG_CHUNK), f32) * SG_CHUNK ** -0.5
    spatial_b = 1.0 + 0.1 * jax.random.normal(ks[11], (L, SG_HEADS, SG_CHUNK), f32)
    w_out = jax.random.normal(ks[12], (L, D_MIX, D_MODEL), f32) * D_MIX ** -0.5
    mlp_w1 = jax.random.normal(ks[13], (L, D_MODEL, D_FF), f32) * D_MODEL ** -0.5
    mlp_w2 = jax.random.normal(ks[14], (L, D_FF, D_MODEL), f32) * D_FF ** -0.5
    final_norm_g = 1.0 + 0.02 * jax.random.normal(ks[15], (D_MODEL,), f32)
    return {"x": x, "c": c, "ada_w": ada_w, "ada_b": ada_b,
            "norm_mix_g": norm_mix_g, "norm_mlp_g": norm_mlp_g, "w_in": w_in,
            "conv_w": conv_w, "conv_b": conv_b, "gmlp_norm_g": gmlp_norm_g,
            "spatial_w": spatial_w, "spatial_b": spatial_b, "w_out": w_out,
            "mlp_w1": mlp_w1, "mlp_w2": mlp_w2, "final_norm_g": final_norm_g}


def reference(x, c, ada_w, ada_b, norm_mix_g, norm_mlp_g, w_in, conv_w, conv_b,
              gmlp_norm_g, spatial_w, spatial_b, w_out, mlp_w1, mlp_w2, final_norm_g):
    bsz, seq, _ = x.shape
    bounds = [SB_WIDTH, 2 * SB_WIDTH, 3 * SB_WIDTH,
              3 * SB_WIDTH + SC_WIDTH, 3 * SB_WIDTH + 2 * SC_WIDTH,
              3 * SB_WIDTH + 3 * SC_WIDTH, 3 * SB_WIDTH + 3 * SC_WIDTH + SG_WIDTH]
    c_act = jax.nn.silu(c)
    for l in range(DEPTH):
        mod = c_act @ ada_w[l] + ada_b[l]
        sh_m, sc_m, g_m, sh_f, sc_f, g_f = jnp.split(mod[:, None, :], N_MOD, axis=-1)

        h = rmsnorm(x, norm_mix_g[l]) * (1.0 + sc_m) + sh_m
        proj = h @ w_in[l]
        q, k, v, b_gate, c_gate, h_conv, u_sg, v_sg = jnp.split(proj, bounds, axis=-1)
        a_out = stick_breaking_attention(
            q.reshape(bsz, seq, SB_HEADS, HEAD_DIM),
            k.reshape(bsz, seq, SB_HEADS, HEAD_DIM),
            v.reshape(bsz, seq, SB_HEADS, HEAD_DIM)).reshape(bsz, seq, SB_WIDTH)
        c_out = short_gated_conv(b_gate, c_gate, h_conv, conv_w[l], conv_b[l])
        s_out = spatial_gating(u_sg, v_sg, gmlp_norm_g[l], spatial_w[l], spatial_b[l])
        mix = jnp.concatenate([a_out, c_out, s_out], axis=-1) @ w_out[l]
        x = x + g_m * mix

        h = rmsnorm(x, norm_mlp_g[l]) * (1.0 + sc_f) + sh_f
        x = x + g_f * (jnp.square(jax.nn.relu(h @ mlp_w1[l])) @ mlp_w2[l])
    return rmsnorm(x, final_norm_g)
```

```python
import contextlib
from collections import deque
import numpy as np
import concourse.bass as bass
import concourse.mybir as mybir
from concourse.bass_utils import run_bass_kernel_spmd

F32 = mybir.dt.float32
BF16 = mybir.dt.bfloat16
AF = mybir.ActivationFunctionType
ALU = mybir.AluOpType
AX = mybir.AxisListType

ENGS = ("pe", "act", "dve", "pool", "sp")
EPS = 1e-6


def I(method, *args, **kw):
    return lambda e: getattr(e, method)(*args, **kw)


class Op:
    __slots__ = ("eng", "fn", "reads", "writes", "dma", "deps", "signal", "sig", "idx", "eidx", "dsem", "dval",
                 "dsem_i")

    def __init__(self, eng, fn, reads, writes, dma):
        self.eng, self.fn, self.reads, self.writes, self.dma = eng, fn, reads, writes, dma
        self.deps = []
        self.signal = False
        self.sig = None
        self.dsem = None
        self.dval = None


class Sched:
    def __init__(self, nc, n_dma_sems=8, self_dist=10 ** 9):
        self.nc = nc
        self.ops = []
        self.lastw = {}
        self.readers = {}
        self.ecount = {e: 0 for e in ENGS}
        self.n_dma_sems = n_dma_sems
        self.self_dist = self_dist

    def op(self, eng, fn, reads=(), writes=(), dma=False):
        o = Op(eng, fn, tuple(reads), tuple(writes), dma)
        o.idx = len(self.ops)
        o.eidx = self.ecount[eng]
        self.ecount[eng] += 1
        deps = set()
        for t in o.reads:
            w = self.lastw.get(t)
            if w is not None:
                deps.add(w)
        for t in o.writes:
            w = self.lastw.get(t)
            if w is not None:
                deps.add(w)
            rd = self.readers.get(t)
            if rd:
                for v in rd[0].values():
                    deps.add(v)
                for v in rd[1]:
                    deps.add(v)
        deps.discard(o.idx)
        o.deps = sorted(deps)
        for t in o.reads:
            rd = self.readers.get(t)
            if rd is None:
                rd = self.readers[t] = ({}, [])
            if dma:
                rd[1].append(o.idx)
            else:
                rd[0][eng] = o.idx
        for t in o.writes:
            self.lastw[t] = o.idx
            self.readers[t] = ({}, [])
        self.ops.append(o)
        return o

    def pe(self, fn, reads=(), writes=()):
        return self.op("pe", fn, reads, writes)

    def act(self, fn, reads=(), writes=()):
        return self.op("act", fn, reads, writes)

    def dve(self, fn, reads=(), writes=()):
        return self.op("dve", fn, reads, writes)

    def pool(self, fn, reads=(), writes=()):
        return self.op("pool", fn, reads, writes)

    def dma(self, q, fn, reads=(), writes=()):
        return self.op(q, fn, reads, writes, dma=True)

    def _needs_sem(self, a, b):
        if a.dma:
            return True
        if a.eng != b.eng:
            return True
        if b.dma:
            return True
        if a.eng == "pe":
            return False
        return (b.eidx - a.eidx) <= self.self_dist

    def emit(self):
        nc = self.nc
        ops = self.ops
        for b in ops:
            for ai in b.deps:
                a = ops[ai]
                if not a.dma and self._needs_sem(a, b):
                    a.signal = True
        with contextlib.ExitStack() as st:
            csem = {e: st.enter_context(nc.semaphore("c_" + e)) for e in ("pe", "act", "dve", "pool")}
            dsems = {}
            for q in ("sp", "pool"):
                dsems[q] = [st.enter_context(nc.semaphore("d_%s%d" % (q, i))) for i in range(self.n_dma_sems)]
            dcnt = {q: [0] * self.n_dma_sems for q in dsems}
            dnext = {q: 0 for q in dsems}
            ccount = {e: 0 for e in csem}
            waited = {e: {} for e in ENGS}
            streams = {e: [] for e in ENGS}
            for o in ops:
                waits = []
                W = waited[o.eng]

                def need(sem, val, key):
                    if W.get(key, 0) < val:
                        W[key] = val
                        waits.append((sem, val))

                for ai in o.deps:
                    a = ops[ai]
                    if a.dma:
                        need(a.dsem, a.dval, ("d", a.eng, a.dsem_i))
                    elif a.signal and self._needs_sem(a, o):
                        need(csem[a.eng], a.sig, ("c", a.eng))
                if o.dma:
                    q = o.eng
                    i = dnext[q]
                    dnext[q] = (i + 1) % self.n_dma_sems
                    sem = dsems[q][i]
                    if dcnt[q][i] > 0:
                        need(sem, dcnt[q][i], ("d", q, i))
                    dcnt[q][i] += 16
                    o.dsem, o.dval = sem, dcnt[q][i]
                    o.dsem_i = i
                    streams[o.eng].append((waits, o.fn, sem, 16))
                else:
                    if o.signal:
                        ccount[o.eng] += 1
                        o.sig = ccount[o.eng]
                        streams[o.eng].append((waits, o.fn, csem[o.eng], 1))
                    else:
                        streams[o.eng].append((waits, o.fn, None, 0))
            finals = []
            for q in dsems:
                for i, sem in enumerate(dsems[q]):
                    if dcnt[q][i] > 0:
                        finals.append((sem, dcnt[q][i]))
            for e in csem:
                if ccount[e] > 0:
                    finals.append((csem[e], ccount[e]))
            self.stats = {e: len(streams[e]) for e in ENGS}
            self.stats["sig"] = dict(ccount)

            def run(engine, lst, extra=None):
                for waits, fn, sem, inc in lst:
                    for (s, v) in waits:
                        engine.wait_ge(s, v)
                    ins = fn(engine)
                    if sem is not None:
                        ins.then_inc(sem, inc)
                if extra:
                    for (s, v) in extra:
                        engine.wait_ge(s, v)

            with nc.Block() as block:
                @block.tensor
                def _(e):
                    run(e, streams["pe"])

                @block.scalar
                def _(e):
                    run(e, streams["act"])

                @block.vector
                def _(e):
                    run(e, streams["dve"])

                @block.gpsimd
                def _(e):
                    run(e, streams["pool"])

                @block.sync
                def _(e):
                    run(e, streams["sp"], extra=finals)


WIN_ORDER = [10, 9, 7, 8, 6, 4, 5, 2, 3, 0, 1]


def build(S, NL, dbg=None):
    NG = S // 512
    NT = S // 128
    nc = bass.Bass("TRN2", target_bir_lowering=False)

    def din(name, shape):
        return nc.dram_tensor(name, shape, F32, kind="ExternalInput").ap()

    x_d = din("x", [S, 1024])
    c_d = din("c", [128, 8])
    adaw_d = din("ada_w", [NL, 1024, 6144])
    adab_d = din("ada_b", [128, NL * 48])
    gmix_d = din("gmix", [128, NL * 8])
    gmlp_d = din("gmlp", [128, NL * 8])
    gfin_d = din("gfin", [128, 8])
    win_d = din("w_in", [NL, 1024, 2816])
    cw_d = din("conv_w", [128, NL * 6])
    cb_d = din("conv_b", [128, NL * 2])
    gng_d = din("gng", [128, NL * 256])
    swT_d = din("swT", [128, NL * 512])
    sb_d = din("sb", [1, NL * 512])
    wout_d = din("w_out", [NL, 1024, 1024])
    w1_d = din("w1", [NL, 1024, 4096])
    w2_d = din("w2", [NL, 4096, 1024])
    out_d = nc.dram_tensor("out", [S, 1024], F32, kind="ExternalOutput").ap()
    dbg_d = None
    if dbg:
        dbg_d = nc.dram_tensor("dbg", [128, 8 * S], F32, kind="ExternalOutput").ap()

    adaw_v = [adaw_d[l].rearrange("(k p) c -> p k c", p=128) for l in range(NL)]
    win_v = [win_d[l].rearrange("(k p) c -> p k c", p=128) for l in range(NL)]
    wout_v = [wout_d[l].rearrange("(k p) c -> p k c", p=128) for l in range(NL)]
    w1_v = [w1_d[l].rearrange("(k p) c -> p k c", p=128) for l in range(NL)]
    w2_v = [w2_d[l].rearrange("(k p) c -> p k c", p=128) for l in range(NL)]
    x_v = x_d.rearrange("(t p) d -> p t d", p=128)
    out_v = out_d.rearrange("(t p) d -> p t d", p=128)

    Hcells = max(8 * NG, 32)
    Qcells = max(12 * NG, 4 * NG + 32)
    if NG == 1:
        Qcells = 36
    off = [0]

    def carve(nbytes):
        o = off[0]
        off[0] += (nbytes + 63) // 64 * 64
        return o

    o_x = carve(8 * S * 4)
    o_H = carve(Hcells * 1024)
    o_Q = carve(Qcells * 1024)
    o_M = carve(8 * S)
    o_S = carve(16 * 1024)
    o_W = carve(16 * 1024)
    o_identf = carve(512)
    o_ctmp = carve(512)
    o_identb = carve(256)
    o_negtri = carve(256)
    o_negones = carve(256)
    o_ones = carve(256)
    o_negm = carve(256)
    o_c = carve(32)
    o_cact = carve(16)
    o_mod = carve(NL * 48 * 4)
    o_adab = carve(NL * 48 * 4)
    o_gmix = carve(NL * 8 * 4)
    o_gmlp = carve(NL * 8 * 4)
    o_gfin = carve(32)
    o_gsm = carve(NL * 8 * 4)
    o_gsf = carve(NL * 8 * 4)
    o_cw = carve(NL * 6 * 4)
    o_cb = carve(NL * 2 * 4)
    o_gng = carve(1024)
    o_gng16 = carve(1024)
    o_swT = carve(1024)
    o_sbb = carve(1024)
    o_modrow = carve(2048)
    o_vstat = carve(256)
    TOTAL = off[0]
    assert TOTAL <= 206 * 1024, TOTAL

    st = contextlib.ExitStack()
    arena = st.enter_context(nc.sbuf_tensor("arena", [128, TOTAL // 2], BF16))
    ps = st.enter_context(nc.psum_tensor("ps", [128, 8, 512], F32))

    def V(o, shape, dt):
        esz = 4 if dt == F32 else 2
        n = int(np.prod(shape))
        ap = arena[:, o // 2: o // 2 + n * esz // 2]
        if dt == F32:
            ap = ap.bitcast(F32)
        if len(shape) == 2:
            ap = ap.rearrange("p (a b) -> p a b", b=shape[1])
        elif len(shape) == 3:
            ap = ap.rearrange("p (a b c) -> p a b c", b=shape[1], c=shape[2])
        return ap

    Hc = lambda c: ("H", c)
    Qc = lambda c: ("Q", c)
    Sc = lambda c: ("S", c)
    Wc = lambda c: ("W", c)

    xT = V(o_x, [8, S], F32)
    hT = V(o_H, [8, S], BF16)
    wo = V(o_W, [8, 1024], BF16)
    Et = [V(o_H + i * 4096, [2, 512], F32) for i in range(2)]
    LT = [V(o_H + (8 + 2 * i) * 1024, [2, 512], BF16) for i in range(3)]
    WT = [V(o_H + (14 + 2 * i) * 1024, [2, 512], BF16) for i in range(3)]
    Rt = V(o_H + 20 * 1024, [2, 512], BF16)
    qT = [V(o_Q + (j * NG) * 1024, [S], BF16) for j in range(4)]
    kT = [V(o_Q + (4 * NG + j * NG) * 1024, [S], BF16) for j in range(4)]
    vv = V(o_Q + 8 * NG * 1024, [NT, 512], BF16)
    h1 = [V(o_Q + sub * NG * 1024, [S], BF16) for sub in range(4)]
    w1s = [V(o_W, [8, 512], BF16), V(o_Q + 4 * NG * 1024, [8, 512], BF16)]
    w2s = [V(o_W + 8192, [4, 1024], BF16), V(o_Q + (4 * NG + 8) * 1024, [4, 1024], BF16)]
    w1tok = [[Wc(0), Wc(1)], [Qc(4 * NG + i) for i in range(8)]]
    w2tok = [[Wc(2), Wc(3)], [Qc(4 * NG + 8 + i) for i in range(8)]]
    ADA0 = 0
    ADA1 = 4 * NG + 16
    XS0 = 16
    xs = [V(o_Q + (XS0 + i * 16) * 1024, [4, 1024], F32) for i in range(2 if NG > 1 else 1)]
    OF0 = 4 * NG + 16
    OS0 = 4 * NG
    of = V(o_Q + OF0 * 1024, [8, 512], F32)
    ost = [V(o_Q + (OS0 + 4 * i) * 1024, [1024], F32) for i in range(2)]
    cout = [V(o_M + j * S * 2, [S], BF16) for j in range(2)]
    sout = [V(o_M + 2 * S * 2 + j * S * 2, [S], BF16) for j in range(2)]
    gvb = V(o_S, [NT, 256], BF16)
    sqv = V(o_S + 8 * 1024, [256], F32)
    sq = [V(o_S + i * 1024, [512], BF16) for i in range(2)]
    rstd = V(o_S + 2048, [512], F32)
    tmp = [V(o_S + 4096 + i * 2048, [512], F32) for i in range(2)]
    Cc = V(o_S, [S], BF16)
    uu = V(o_S + 4096, [S + 2], F32)
    ytmp = V(o_S + 13 * 1024, [512], F32)
    wt = [V(o_W + i * 4096, [8, 256], BF16) for i in range(4)]
    ident_f = V(o_identf, [128], F32)
    ctmp = V(o_ctmp, [128], F32)
    ident_b = V(o_identb, [128], BF16)
    negtri = V(o_negtri, [128], BF16)
    negones = V(o_negones, [128], BF16)
    ones_b = V(o_ones, [128], BF16)
    negm = V(o_negm, [128], BF16)
    c_sb = V(o_c, [8], F32)
    cact = V(o_cact, [8], BF16)
    mod = V(o_mod, [NL * 48], F32)
    adab = V(o_adab, [NL * 48], F32)
    gmix = V(o_gmix, [NL * 8], F32)
    gmlp = V(o_gmlp, [NL * 8], F32)
    gfin = V(o_gfin, [8], F32)
    gsm = V(o_gsm, [NL * 8], F32)
    gsf = V(o_gsf, [NL * 8], F32)
    cw = V(o_cw, [NL * 6], F32)
    cb = V(o_cb, [NL * 2], F32)
    gng = V(o_gng, [256], F32)
    gng16 = V(o_gng16, [256], F32)
    swT = V(o_swT, [4, 128], BF16)
    sbb = V(o_sbb, [512], BF16)
    modrow = [V(o_modrow, [512], F32)]
    vstat = V(o_vstat, [64], F32)

    sc = Sched(nc)
    bank_rr = [0]

    def nb():
        b = bank_rr[0]
        bank_rr[0] = (b + 1) % 8
        return b

    bg = deque()

    def bg_step(n=1):
        for _ in range(n):
            if bg:
                bg.popleft()()

    sc.pool(I("memset", ident_f, 1.0), writes=["identf"])
    sc.pool(I("affine_select", out=ident_f, in_=ident_f, pattern=[[-1, 128]], compare_op=ALU.is_equal,
                                      fill=0.0, base=0, channel_multiplier=1), reads=["identf"], writes=["identf"])
    sc.dve(I("tensor_copy", out=ident_b, in_=ident_f), reads=["identf"], writes=["identb"])
    sc.pool(I("memset", ctmp, -1.0), writes=["ctmp"])
    sc.pool(I("affine_select", out=ctmp, in_=ctmp, pattern=[[-1, 128]], compare_op=ALU.is_ge,
                                      fill=0.0, base=0, channel_multiplier=1), reads=["ctmp"], writes=["ctmp"])
    sc.dve(I("tensor_copy", out=negtri, in_=ctmp), reads=["ctmp"], writes=["negtri"])
    sc.pool(I("memset", ctmp, -30000.0), reads=["ctmp"], writes=["ctmp"])
    sc.pool(I("affine_select", out=ctmp, in_=ctmp, pattern=[[-1, 128]], compare_op=ALU.is_ge,
                                      fill=0.0, base=0, channel_multiplier=1), reads=["ctmp"], writes=["ctmp"])
    sc.dve(I("tensor_copy", out=negm, in_=ctmp), reads=["ctmp"], writes=["negm"])
    sc.pool(I("memset", negones, -1.0), writes=["negones"])
    sc.pool(I("memset", ones_b, 1.0), writes=["ones"])
    for (dst, src, tok) in ((c_sb, c_d, "c"), (adab, adab_d, "adab"), (gmix, gmix_d, "gmix"), (gmlp, gmlp_d, "gmlp"),
                            (gfin, gfin_d, "gfin"), (cw, cw_d, "cw"), (cb, cb_d, "cb")):
        sc.dma("sp", I("dma_start", out=dst, in_=src), writes=[tok])
    sc.act(I("activation", out=cact, in_=c_sb, func=AF.Silu), reads=["c"], writes=["cact"])

    def ada_setup(l, base):
        tiles = [V(o_Q + (base + 8 * i) * 1024, [8, 512], BF16) for i in range(2)]
        ttok = [[Qc(base + 8 * i + c) for c in range(8)] for i in range(2)]

        def issue(ct):
            sc.dma("pool", I("dma_start", out=tiles[ct % 2], in_=adaw_v[l][:, :, ct * 512:(ct + 1) * 512]),
                   writes=ttok[ct % 2])

        def item(ct):
            def f():
                if ct + 1 < 12:
                    issue(ct + 1)
                b = nb()
                for k in range(8):
                    sc.pe(I("matmul", ps[0:1, b, :], lhsT=cact[:, k:k + 1], rhs=tiles[ct % 2][:, k, :],
                                                  start=(k == 0), stop=(k == 7)),
                          reads=ttok[ct % 2] + ["cact"], writes=[("ps", b)])
                mr = modrow[0]
                sc.dve(I("tensor_copy", out=mr[0:1, :], in_=ps[0:1, b, :]), reads=[("ps", b)],
                       writes=[("modrow", 0)])
                b2 = nb()
                for jj in range(4):
                    sc.pe(I("matmul", ps[:, b2, jj:jj + 1], lhsT=mr[0:1, jj * 128:(jj + 1) * 128],
                                                    rhs=ident_f[0:1, 0:1], start=True, stop=True),
                          reads=[("modrow", 0), "identf"], writes=[("ps", b2)])
                c0 = l * 48 + ct * 4
                sc.dve(I("tensor_tensor", out=mod[:, c0:c0 + 4], in0=ps[:, b2, 0:4], in1=adab[:, c0:c0 + 4],
                                                 op=ALU.add), reads=[("ps", b2), "adab"], writes=[("mod", l, ct)])
                if ct == 3:
                    sc.dve(I("scalar_tensor_tensor", out=gsm[:, l * 8:l * 8 + 8], in0=mod[:, l * 48 + 8:l * 48 + 16],
                                                            scalar=1.0, in1=gmix[:, l * 8:l * 8 + 8], op0=ALU.add,
                                                            op1=ALU.mult),
                           reads=[("mod", l, 2), ("mod", l, 3), "gmix"], writes=[("gsm", l)])
                if ct == 9:
                    sc.dve(I("scalar_tensor_tensor", out=gsf[:, l * 8:l * 8 + 8], in0=mod[:, l * 48 + 32:l * 48 + 40],
                                                            scalar=1.0, in1=gmlp[:, l * 8:l * 8 + 8], op0=ALU.add,
                                                            op1=ALU.mult),
                           reads=[("mod", l, 8), ("mod", l, 9), "gmlp"], writes=[("gsf", l)])
            return f

        issue(0)
        return [item(ct) for ct in range(12)]

    def modtok(l, c0):
        return [("mod", l, c0 // 4), ("mod", l, c0 // 4 + 1)]

    def emit_norm(gs_ap, sh_ap, gtoks):
        for g in range(NG):
            norm_group(g, gs_ap, sh_ap, gtoks)

    def norm_group(g, gs_ap, sh_ap, gtoks):
        if True:
            b = nb()
            gsl = slice(g * 512, (g + 1) * 512)
            for k in range(8):
                sc.act(I("activation", out=sq[k % 2], in_=xT[:, k, gsl], func=AF.Square),
                       reads=[("x", k, g)], writes=[Sc(k % 2)])
                sc.pe(I("matmul", ps[:, b, :], lhsT=ones_b, rhs=sq[k % 2], start=(k == 0), stop=(k == 7)),
                      reads=[Sc(k % 2), "ones"], writes=[("ps", b)])
            sc.act(I("activation", out=rstd, in_=ps[:, b, :], func=AF.Ln, scale=1.0 / 1024, bias=EPS),
                   reads=[("ps", b)], writes=[Sc(2), Sc(3)])
            sc.act(I("activation", out=rstd, in_=rstd, func=AF.Exp, scale=-0.5),
                   reads=[Sc(2), Sc(3)], writes=[Sc(2), Sc(3)])
            for k in range(8):
                tt = tmp[k % 2]
                ttk = [Sc(4 + 2 * (k % 2)), Sc(5 + 2 * (k % 2))]
                sc.dve(I("scalar_tensor_tensor", out=tt, in0=xT[:, k, gsl], scalar=gs_ap[:, k:k + 1],
                                                                    in1=rstd, op0=ALU.mult, op1=ALU.mult),
                       reads=[("x", k, g), Sc(2), Sc(3)] + gtoks, writes=ttk)
                if k % 2 == 0:
                    sc.act(I("activation", out=hT[:, k, gsl], in_=tt, func=AF.Identity, bias=sh_ap[:, k:k + 1], scale=1.0),
                           reads=ttk + gtoks, writes=[Hc(k * NG + g)])
                else:
                    sc.dve(I("tensor_scalar", out=hT[:, k, gsl], in0=tt, scalar1=sh_ap[:, k:k + 1], scalar2=None,
                             op0=ALU.add), reads=ttk + gtoks, writes=[Hc(k * NG + g)])

    def win_issue(l, n):
        t = WIN_ORDER[n]
        slot = n % 4
        sc.dma("pool", I("dma_start", out=wt[slot], in_=win_v[l][:, :, t * 256:(t + 1) * 256]),
               writes=[Wc(slot)])

    def fm_group(slot, sub, g, evac):
        b = nb()
        for k in range(8):
            sc.pe(I("matmul", ps[:, b, :], lhsT=wt[slot][:, k, sub * 128:(sub + 1) * 128],
                                          rhs=hT[:, k, g * 512:(g + 1) * 512], start=(k == 0), stop=(k == 7)),
                  reads=[Wc(slot), Hc(k * NG + g)], writes=[("ps", b)])
        evac(b)

    def tm_group(slot, tt, evac):
        b = nb()
        for k in range(8):
            sc.pe(I("matmul", ps[:, b, 0:256], lhsT=hT[:, k, tt * 128:(tt + 1) * 128],
                                          rhs=wt[slot][:, k, :], start=(k == 0), stop=(k == 7)),
                  reads=[Wc(slot), Hc(k * NG + tt // 4)], writes=[("ps", b)])
        evac(b)

    def emit_proj(l, norm_hook=None):
        sc.dma("sp", I("dma_start", out=gng, in_=gng_d[:, l * 256:(l + 1) * 256]), writes=["gng"])
        sc.dma("pool", I("dma_start", out=swT, in_=swT_d[:, l * 512:(l + 1) * 512].rearrange("p (a b) -> p a b", b=128)),
               writes=["swT"])
        sc.pool(I("memset", swT[64:128, :, 0:64], 0.0), reads=["swT"], writes=["swT"])
        sc.dma("pool", I("dma_start", out=sbb[0:1, :], in_=sb_d[0:1, l * 512:(l + 1) * 512]), writes=["sbb"])
        for n in range(11):
            t = WIN_ORDER[n]
            slot = n % 4
            if t == 9:
                for sub in range(2):
                    for g in range(NG):
                        def ev(b, sub=sub, g=g):
                            sc.act(I("activation", out=sout[sub][:, g * 512:(g + 1) * 512], in_=ps[:, b, :],
                                     func=AF.Gelu_apprx_tanh), reads=[("ps", b)], writes=[("so", sub, g)])
                        fm_group(slot, sub, g, ev)
                vst = [("vs", tt) for tt in range(NT)]
                sc.act(I("activation", out=vstat[:, NT:2 * NT], in_=vstat[:, 0:NT], func=AF.Ln, scale=1.0 / 256, bias=EPS),
                       reads=vst, writes=["vs2"])
                sc.act(I("activation", out=vstat[:, NT:2 * NT], in_=vstat[:, NT:2 * NT], func=AF.Exp, scale=-0.5),
                       reads=["vs2"], writes=["vs2"])
                def scale_tile(tt):
                    sc.dve(I("scalar_tensor_tensor", out=gvb[:, tt, :], in0=gvb[:, tt, :], scalar=vstat[:, NT + tt:NT + tt + 1],
                             in1=gng16, op0=ALU.mult, op1=ALU.mult),
                           reads=[("gvb", tt), Sc(tt // 2), "vs2", "gng16"], writes=[("gvb", tt)])

                def mm_tile(tt):
                    b2 = nb()
                    for hh in range(4):
                        j = hh // 2
                        sc.pe(I("matmul", ps[:, b2, hh * 128:(hh + 1) * 128], lhsT=gvb[:, tt, j * 128:(j + 1) * 128],
                                rhs=swT[:, hh, :], start=True, stop=False),
                              reads=[("gvb", tt), Sc(tt // 2), "swT"], writes=[("ps", b2)])
                        sc.pe(I("matmul", ps[:, b2, hh * 128:(hh + 1) * 128], lhsT=ones_b[0:1, :],
                                rhs=sbb[0:1, hh * 128:(hh + 1) * 128], start=False, stop=True),
                              reads=["sbb", "ones"], writes=[("ps", b2)])
                    return b2

                def evac_tile(tt, b2):
                    g = tt // 4
                    tsl = slice(tt * 128, (tt + 1) * 128)
                    for hh in range(4):
                        j = hh // 2
                        rb = 64 * (hh % 2)
                        sc.dve(I("tensor_tensor", out=sout[j][rb:rb + 64, tsl], in0=sout[j][rb:rb + 64, tsl],
                                 in1=ps[rb:rb + 64, b2, hh * 128:(hh + 1) * 128], op=ALU.mult),
                               reads=[("ps", b2), ("so", j, g)], writes=[("so", j, g)])

                scale_tile(0)
                if NT > 1:
                    scale_tile(1)
                pend = None
                for tt in range(NT):
                    b2 = mm_tile(tt)
                    if tt + 2 < NT:
                        scale_tile(tt + 2)
                    if pend is not None:
                        evac_tile(*pend)
                    pend = (tt, b2)
                evac_tile(*pend)
            elif t == 10:
                sc.dve(I("tensor_scalar", out=gng16, in0=gng, scalar1=1.0, scalar2=None, op0=ALU.mult),
                       reads=["gng"], writes=["gng16"])
                for tt in range(NT):
                    if norm_hook is not None:
                        norm_hook(tt)

                    def evA(b, tt=tt):
                        sc.act(I("activation", out=gvb[:, tt, :], in_=ps[:, b, 0:256], func=AF.Gelu_apprx_tanh),
                               reads=[("ps", b)], writes=[Sc(tt // 2), ("gvb", tt)])
                        sc.dve(I("tensor_tensor", out=sqv, in0=gvb[:, tt, :], in1=gvb[:, tt, :], op=ALU.mult),
                               reads=[("gvb", tt), Sc(tt // 2)], writes=[Sc(8)])
                        sc.dve(I("reduce_sum", out=vstat[:, tt:tt + 1], in_=sqv, axis=AX.X),
                               reads=[Sc(8)], writes=[("vs", tt)])
                    tm_group(slot, tt, evA)
            elif t == 7:
                pass
            elif t == 8:
                pass
            elif t == 6:
                sC, sH, sB = (n - 2) % 4, (n - 1) % 4, n % 4
                for j in range(2):
                    for g in range(NG):
                        def evC(b, g=g):
                            sc.act(I("activation", out=Cc[:, g * 512:(g + 1) * 512], in_=ps[:, b, :], func=AF.Copy),
                                   reads=[("ps", b)], writes=[Sc(g)])
                        fm_group(sC, j, g, evC)
                    sc.dve(I("memset", uu[:, 0:2], 0.0), writes=[Sc(4)])
                    for g in range(NG):
                        def evH(b, g=g):
                            utok = [Sc(4 + 2 * g + i) for i in range(3)]
                            sc.dve(I("tensor_tensor", out=uu[:, 2 + g * 512:2 + (g + 1) * 512], in0=ps[:, b, :],
                                                             in1=Cc[:, g * 512:(g + 1) * 512], op=ALU.mult),
                                   reads=[("ps", b), Sc(g)], writes=utok)
                        fm_group(sH, j, g, evH)
                    for g in range(NG):
                        def evB(b, g=g, j=j):
                            utok = [Sc(4 + 2 * g + i) for i in range(-2, 3) if 4 + 2 * g + i >= 4]
                            c6 = l * 6 + j * 3
                            ytok = [Sc(13), Sc(14)]
                            sc.dve(I("tensor_scalar", out=ytmp, in0=uu[:, 2 + g * 512:2 + (g + 1) * 512],
                                                             scalar1=cw[:, c6 + 2:c6 + 3], scalar2=None, op0=ALU.mult),
                                   reads=utok + ["cw"], writes=ytok)
                            sc.dve(I("scalar_tensor_tensor", out=ytmp, in0=uu[:, 1 + g * 512:1 + (g + 1) * 512],
                                                                    scalar=cw[:, c6 + 1:c6 + 2], in1=ytmp, op0=ALU.mult,
                                                                    op1=ALU.add),
                                   reads=utok + ytok, writes=ytok)
                            sc.dve(I("scalar_tensor_tensor", out=ytmp, in0=uu[:, g * 512:(g + 1) * 512],
                                                                    scalar=cw[:, c6:c6 + 1], in1=ytmp, op0=ALU.mult,
                                                                    op1=ALU.add),
                                   reads=utok + ytok, writes=ytok)
                            sc.dve(I("scalar_tensor_tensor", out=cout[j][:, g * 512:(g + 1) * 512], in0=ytmp,
                                                                    scalar=cb[:, l * 2 + j:l * 2 + j + 1], in1=ps[:, b, :],
                                                                    op0=ALU.add, op1=ALU.mult),
                                   reads=ytok + [("ps", b), "cb"], writes=[("co", j, g)])
                        fm_group(sB, j, g, evB)
            elif t in (4, 5):
                half = t - 4
                for tt in range(NT):
                    def ev(b, tt=tt):
                        sc.dve(I("tensor_copy", out=vv[:, tt, half * 256:(half + 1) * 256], in_=ps[:, b, 0:256]),
                               reads=[("ps", b)], writes=[Qc(8 * NG + tt)])
                    tm_group(slot, tt, ev)
            elif t in (2, 3):
                for sub in range(2):
                    j = (t - 2) * 2 + sub
                    for g in range(NG):
                        def ev(b, j=j, g=g):
                            sc.act(I("activation", out=kT[j][:, g * 512:(g + 1) * 512], in_=ps[:, b, :], func=AF.Copy),
                                   reads=[("ps", b)], writes=[Qc(4 * NG + j * NG + g)])
                        fm_group(slot, sub, g, ev)
            else:
                if t == 0 and n == 9:
                    pass
                for sub in range(2):
                    j = t * 2 + sub
                    for g in range(NG):
                        def ev(b, j=j, g=g):
                            sc.dve(I("tensor_scalar", out=qT[j][:, g * 512:(g + 1) * 512], in0=ps[:, b, :],
                                                             scalar1=0.125, scalar2=None, op0=ALU.mult),
                                   reads=[("ps", b)], writes=[Qc(j * NG + g)])
                        fm_group(slot, sub, g, ev)
            if t in (7, 8):
                continue
            if t == 6:
                for nn in (n - 2, n - 1, n):
                    if nn + 4 < 11:
                        win_issue(l, nn + 4)
            elif n + 4 < 11:
                win_issue(l, n + 4)
            if n == 8:
                bg_step(len(bg))
            else:
                bg_step(2 if n == 0 else 1)

    def emit_attn(l):
        items = []
        for j in range(4):
            for m in range(NG):
                for c in range(4 * m + 3, -1, -1):
                    items.append((j, m, c))
        n = len(items)

        def geom(it):
            j, m, c = it
            r = c - 4 * m
            diag = r >= 0
            v0 = 128 * r if diag else 0
            return j, m, c, diag, v0

        def ztok(i):
            return [("ps", 2 * (i % 3)), ("ps", 2 * (i % 3) + 1)]

        def stageA(i):
            j, m, c, diag, v0 = geom(items[i])
            t0 = m * 512
            ktok = Qc(4 * NG + j * NG + c // 4)
            qtok = Qc(j * NG + m)
            for hp in range(2):
                rb = 64 * hp
                zb = 2 * (i % 3) + hp
                kk = kT[j][rb:rb + 64, c * 128:(c + 1) * 128]
                if diag:
                    rest = v0 + 128 < 512
                    if rest:
                        sc.pe(I("matmul", ps[:, zb, v0 + 128:512], lhsT=kk, rhs=qT[j][rb:rb + 64, t0 + v0 + 128:t0 + 512],
                                start=True, stop=False, skip_group_check=True), reads=[ktok, qtok], writes=[("ps", zb)])
                    sc.pe(I("matmul", ps[:, zb, v0:v0 + 128], lhsT=kk, rhs=qT[j][rb:rb + 64, t0 + v0:t0 + v0 + 128],
                            start=not rest, stop=False, skip_group_check=True), reads=[ktok, qtok], writes=[("ps", zb)])
                    sc.pe(I("matmul", ps[:, zb, v0:v0 + 128], lhsT=ident_b, rhs=negm, start=False, stop=True,
                            skip_group_check=True), reads=["identb", "negm"], writes=[("ps", zb)])
                else:
                    sc.pe(I("matmul", ps[:, zb, :], lhsT=kk, rhs=qT[j][rb:rb + 64, t0:t0 + 512], start=True, stop=True,
                            skip_group_check=True), reads=[ktok, qtok], writes=[("ps", zb)])

        def stageB(i):
            j, m, c, diag, v0 = geom(items[i])
            z0 = 2 * (i % 3)
            E = Et[i % 2]
            etok = [Hc(4 * (i % 2) + q) for q in range(4)]
            L = LT[i % 3]
            ltok = [Hc(8 + 2 * (i % 3)), Hc(9 + 2 * (i % 3))]
            sc.act(I("activation", out=E[:, :, v0:512], in_=ps[:, z0:z0 + 2, v0:512], func=AF.Exp),
                   reads=ztok(i), writes=etok)
            sc.act(I("activation", out=L[:, :, v0:512], in_=E[:, :, v0:512], func=AF.Ln, bias=1.0),
                   reads=etok, writes=ltok)

        def stageC(i):
            j, m, c, diag, v0 = geom(items[i])
            z0 = 2 * (i % 3)
            L = LT[i % 3]
            ltok = [Hc(8 + 2 * (i % 3)), Hc(9 + 2 * (i % 3))]
            wtok = [Hc(14 + 2 * (i % 3)), Hc(15 + 2 * (i % 3))]
            rtok = [Hc(20), Hc(21)]
            cmax = 4 * m + 3
            r0 = v0 + 128 if diag else 0
            hasR = (c < cmax) and r0 < 512
            for hp in range(2):
                zb = z0 + hp
                sc.pe(I("matmul", ps[:, zb, v0:512], lhsT=negtri, rhs=L[:, hp, v0:512], start=False, stop=not hasR,
                        skip_group_check=True), reads=ltok + ["negtri"], writes=[("ps", zb)])
                if hasR:
                    sc.pe(I("matmul", ps[:, zb, r0:512], lhsT=negones, rhs=Rt[:, hp, r0:512], start=False, stop=True,
                            skip_group_check=True), reads=rtok + ["negones"], writes=[("ps", zb)])
            sc.act(I("activation", out=WT[i % 3][:, :, v0:512], in_=ps[:, z0:z0 + 2, v0:512], func=AF.Exp),
                   reads=ztok(i), writes=wtok)
            if c > 0:
                if c == cmax:
                    sc.pool(I("memset", Rt, 0.0), writes=rtok)
                sc.pool(I("tensor_tensor", out=Rt[:, :, v0:512], in0=Rt[:, :, v0:512], in1=L[:, :, v0:512], op=ALU.add),
                        reads=rtok + ltok, writes=rtok)

        def stageE(i):
            j, m, c, diag, v0 = geom(items[i])
            W = WT[i % 3]
            wtok = [Hc(14 + 2 * (i % 3)), Hc(15 + 2 * (i % 3))]
            vtok = Qc(8 * NG + c)
            last = (c == 0)
            first = (c == 4 * m + 3)
            for hp in range(2):
                sc.pe(I("matmul", ps[64 * hp:64 * hp + 64, 6, v0:512], lhsT=vv[:, c, (2 * j + hp) * 64:(2 * j + hp + 1) * 64],
                        rhs=W[:, hp, v0:512], start=first, stop=last, skip_group_check=True),
                      reads=[vtok] + wtok, writes=[("ps", 6)])
            if last:
                sc.dve(I("tensor_copy", out=qT[j][:, m * 512:(m + 1) * 512], in_=ps[:, 6, :]),
                       reads=[("ps", 6)], writes=[Qc(j * NG + m)])
                for ct in range(8):
                    fillers.append((l, [j], m, ct))

        for g in range(NG):
            for ct in range(8):
                fillers.append((l, [4, 5], g, ct))
                fillers.append((l, [6, 7], g, ct))

        popped = [0]
        early = [False, False]
        for i in range(n + 2):
            if i < n:
                stageA(i)
                stageB(i)
            if i >= 4:
                pop_filler(1)
                popped[0] += 1
                if popped[0] == 16 * NG + 2 and not early[1]:
                    issue_w2(l, 0)
                    early[1] = True
            if NG >= 4 and i < n and items[i][0] == 2 and items[i][1] == 0 and items[i][2] == 3:
                issue_w1(l, 0)
                early[0] = True
            if 0 <= i - 1 < n:
                stageC(i - 1)
            if 0 <= i - 2 < n:
                stageE(i - 2)
        return early

    fillers = deque()

    def wout_partial(l, kts, g, ct, bank=7):
        mix = [qT[0], qT[1], qT[2], qT[3], cout[0], cout[1], sout[0], sout[1]]

        def mtok(kt):
            if kt < 4:
                return Qc(kt * NG + g)
            if kt < 6:
                return ("co", kt - 4, g)
            return ("so", kt - 6, g)
        for q, kt in enumerate(kts):
            sc.pe(I("matmul", ps[:, bank, :], lhsT=wo[:, kt, ct * 128:(ct + 1) * 128], rhs=mix[kt][:, g * 512:(g + 1) * 512],
                    start=(q == 0), stop=(q == len(kts) - 1)),
                  reads=[Wc(kt // 2), mtok(kt)], writes=[("ps", bank)])
        sc.dve(I("scalar_tensor_tensor", out=xT[:, ct, g * 512:(g + 1) * 512], in0=ps[:, bank, :],
                 scalar=mod[:, l * 48 + 16 + ct:l * 48 + 17 + ct], in1=xT[:, ct, g * 512:(g + 1) * 512],
                 op0=ALU.mult, op1=ALU.add),
               reads=[("ps", bank), ("x", ct, g)] + modtok(l, 16), writes=[("x", ct, g)])

    def pop_filler(k=1):
        for _ in range(k):
            if fillers:
                wout_partial(*fillers.popleft())

    def emit_wout(l):
        def nf(g):
            norm_group(g, gsf[:, l * 8:(l + 1) * 8], mod[:, l * 48 + 24:l * 48 + 32], [("gsf", l)] + modtok(l, 24))
        pending = sorted(set(f[2] for f in fillers))
        for g in range(NG):
            if g not in pending:
                nf(g)
        while fillers:
            wout_partial(*fillers.popleft(), bank=nb())
        for g in pending:
            nf(g)

    def issue_w1(l, ft):
        if ft < 8:
            s1 = (ft + 1) % 2
            sc.dma("pool", I("dma_start", out=w1s[s1], in_=w1_v[l][:, :, ft * 512:(ft + 1) * 512]), writes=w1tok[s1])

    def issue_w2(l, ft):
        if ft < 8:
            s2 = ft % 2
            sc.dma("pool", I("dma_start", out=w2s[s2], in_=w2_v[l][:, ft * 4:(ft + 1) * 4, :]), writes=w2tok[s2])

    def emit_ffn(l, tail):
        for ft in range(8):
            s1 = (ft + 1) % 2
            s2 = ft % 2
            issue_w2(l, ft + 1)
            order1 = [(sub, g) for sub in range(4) for g in range(NG)]
            if ft == 0:
                order1 = [(sub, g) for g in range(NG) for sub in range(4)]
            for (sub, g) in order1:
                b = nb()
                for k in range(8):
                    sc.pe(I("matmul", ps[:, b, :], lhsT=w1s[s1][:, k, sub * 128:(sub + 1) * 128],
                            rhs=hT[:, k, g * 512:(g + 1) * 512], start=(k == 0), stop=(k == 7)),
                          reads=w1tok[s1] + [Hc(k * NG + g)], writes=[("ps", b)])
                tq = tmp[b % 2]
                tqk = [Sc(4 + 2 * (b % 2)), Sc(5 + 2 * (b % 2))]
                sc.act(I("activation", out=tq, in_=ps[:, b, :], func=AF.Square), reads=[("ps", b)], writes=tqk)
                sc.dve(I("scalar_tensor_tensor", out=h1[sub][:, g * 512:(g + 1) * 512], in0=ps[:, b, :], scalar=0.0,
                         in1=tq, op0=ALU.is_gt, op1=ALU.mult), reads=[("ps", b)] + tqk, writes=[Qc(sub * NG + g)])
            bg_step(1)
            issue_w1(l, ft + 2)
            if ft == 7 and l + 1 < NL:
                for n in range(4):
                    win_issue(l + 1, n)
            order2 = [(ct, g) for ct in range(8) for g in range(NG)]
            if ft == 7:
                order2 = [(ct, g) for g in range(NG) for ct in range(8)]
            for (ct, g) in order2:
                b = nb()
                for kk in range(4):
                    sc.pe(I("matmul", ps[:, b, :], lhsT=w2s[s2][:, kk, ct * 128:(ct + 1) * 128],
                            rhs=h1[kk][:, g * 512:(g + 1) * 512], start=(kk == 0), stop=(kk == 3)),
                          reads=w2tok[s2] + [Qc(kk * NG + g)], writes=[("ps", b)])
                sc.dve(I("scalar_tensor_tensor", out=xT[:, ct, g * 512:(g + 1) * 512], in0=ps[:, b, :],
                         scalar=mod[:, l * 48 + 40 + ct:l * 48 + 41 + ct], in1=xT[:, ct, g * 512:(g + 1) * 512],
                         op0=ALU.mult, op1=ALU.add),
                       reads=[("ps", b), ("x", ct, g)] + modtok(l, 40), writes=[("x", ct, g)])
                if ft == 7 and ct == 7 and g >= 1:
                    tail(g - 1)
            if ft == 7:
                tail(NG - 1)
            bg_step(1)

    def final_group(g):
        if True:
            b = nb()
            gsl = slice(g * 512, (g + 1) * 512)
            for k in range(8):
                sc.act(I("activation", out=sq[k % 2], in_=xT[:, k, gsl], func=AF.Square),
                       reads=[("x", k, g)], writes=[Sc(k % 2)])
                sc.pe(I("matmul", ps[:, b, :], lhsT=ones_b, rhs=sq[k % 2], start=(k == 0), stop=(k == 7)),
                      reads=[Sc(k % 2), "ones"], writes=[("ps", b)])
            sc.act(I("activation", out=rstd, in_=ps[:, b, :], func=AF.Ln, scale=1.0 / 1024, bias=EPS),
                   reads=[("ps", b)], writes=[Sc(2), Sc(3)])
            sc.act(I("activation", out=rstd, in_=rstd, func=AF.Exp, scale=-0.5),
                   reads=[Sc(2), Sc(3)], writes=[Sc(2), Sc(3)])
            for k in range(8):
                sc.dve(I("scalar_tensor_tensor", out=of[:, k, :], in0=xT[:, k, gsl], scalar=gfin[:, k:k + 1],
                                                             in1=rstd, op0=ALU.mult, op1=ALU.mult),
                       reads=[("x", k, g), Sc(2), Sc(3), "gfin"], writes=[Qc(OF0 + 2 * k), Qc(OF0 + 2 * k + 1)])
            for i in range(4):
                tt = g * 4 + i
                oi = tt % 2
                otok = [Qc(OS0 + 4 * oi + c) for c in range(4)]
                for half in range(2):
                    b2 = nb()
                    for kq in range(4):
                        k = half * 4 + kq
                        sc.pe(I("transpose", ps[:, b2, kq * 128:(kq + 1) * 128],
                                                                of[:, k, i * 128:(i + 1) * 128], ident_f),
                              reads=[Qc(OF0 + 2 * k), Qc(OF0 + 2 * k + 1), "identf"], writes=[("ps", b2)])
                    if half == 0:
                        sc.act(I("activation", out=ost[oi][:, 0:512], in_=ps[:, b2, :], func=AF.Copy),
                               reads=[("ps", b2)], writes=otok)
                    else:
                        sc.dve(I("tensor_copy", out=ost[oi][:, 512:1024], in_=ps[:, b2, :]),
                               reads=[("ps", b2)], writes=otok)
                sc.dma("sp", I("dma_start", out=out_v[:, tt, :], in_=ost[oi]), reads=otok)

    def dump(items):
        o = 0
        for ap, toks, n in items:
            sc.dma("pool", I("dma_start", out=dbg_d[:, o:o + n], in_=ap), reads=toks)
            o += n

    def xitems():
        return [(xT[:, k, :], [("x", k, g) for g in range(NG)], S) for k in range(8)]

    ada0 = ada_setup(0, ADA0)
    n_up = 4 if NG >= 4 else 12

    def norm_m_group(l, g):
        norm_group(g, gsm[:, l * 8:(l + 1) * 8], mod[:, l * 48:l * 48 + 8], [("gsm", l)] + modtok(l, 0))

    def xload(g):
        xi = g % len(xs)
        xtok = [Qc(XS0 + xi * 16 + c) for c in range(16)]
        sc.dma("sp", I("dma_start", out=xs[xi], in_=x_v[:, g * 4:(g + 1) * 4, :]), writes=xtok)
        for k in range(8):
            b = nb()
            for i in range(4):
                sc.pe(I("transpose", ps[:, b, i * 128:(i + 1) * 128], xs[xi][:, i, k * 128:(k + 1) * 128], ident_f),
                      reads=xtok + ["identf"], writes=[("ps", b)])
            if k % 2 == 0:
                sc.act(I("activation", out=xT[:, k, g * 512:(g + 1) * 512], in_=ps[:, b, :], func=AF.Copy),
                       reads=[("ps", b)], writes=[("x", k, g)])
            else:
                sc.dve(I("tensor_copy", out=xT[:, k, g * 512:(g + 1) * 512], in_=ps[:, b, :]),
                       reads=[("ps", b)], writes=[("x", k, g)])

    up = list(ada0[:n_up])
    half = (len(up) + 1) // 2
    for g in range(NG):
        xload(g)
        if g == 0:
            for f in up[:half]:
                f()
        if g == min(1, NG - 1):
            for f in up[half:] if NG > 1 else up[half:]:
                f()
        if g >= 2:
            norm_m_group(0, g - 2)
    for f in ada0[n_up:]:
        bg.append(f)
    for n in range(4):
        win_issue(0, n)
    for g in range(max(NG - 2, 0), NG):
        norm_m_group(0, g)
    l0_hook = None
    if dbg == "x0":
        dump(xitems())
    for l in range(NL):
        if dbg == "h0" and l == 0:
            dump([(hT[:, k, :], [Hc(k * NG + g) for g in range(NG)], S) for k in range(8)])
        emit_proj(l, None)
        bg_step(len(bg))
        if dbg == "qk" and l == 0:
            dump([(qT[j], [Qc(j * NG + g) for g in range(NG)], S) for j in range(4)] +
                 [(kT[j], [Qc(4 * NG + j * NG + g) for g in range(NG)], S) for j in range(4)])
        if dbg == "v" and l == 0:
            dump([(vv[:, tt, :], [Qc(8 * NG + tt)], 512) for tt in range(NT)])
        if dbg == "cs" and l == 0:
            dump([(cout[j], [("co", j, g) for g in range(NG)], S) for j in range(2)] +
                 [(sout[j], [("so", j, g) for g in range(NG)], S) for j in range(2)])
        for q4 in range(4):
            sc.dma("pool", I("dma_start", out=wo[:, 2 * q4:2 * q4 + 2, :], in_=wout_v[l][:, 2 * q4:2 * q4 + 2, :]),
                   writes=[Wc(q4)])
        early = emit_attn(l)
        if not early[0]:
            issue_w1(l, 0)
        if dbg == "att" and l == 0:
            dump([(qT[j], [Qc(j * NG + g) for g in range(NG)], S) for j in range(4)])
        emit_wout(l)
        if not early[1]:
            issue_w2(l, 0)
        issue_w1(l, 1)
        if dbg == "x1" and l == 0:
            dump(xitems())
        if l + 1 < NL:
            for f in ada_setup(l + 1, ADA1):
                bg.append(f)
            emit_ffn(l, lambda g, l=l: norm_m_group(l + 1, g))
        elif dbg:
            emit_ffn(l, lambda g: None)
        else:
            emit_ffn(l, final_group)
        bg_step(len(bg))
    if dbg:
        if dbg == "x":
            dump(xitems())
        for g in range(NG):
            final_group(g)
    sc.emit()
    st.close()
    return nc, sc.stats


def _layout(inputs, b, NL):
    f = lambda a: np.ascontiguousarray(a, dtype=np.float32)
    d = {}
    d["x"] = f(inputs["x"][b])
    d["c"] = f(inputs["c"][b].reshape(8, 128).T)
    d["ada_w"] = f(inputs["ada_w"][:NL])
    d["ada_b"] = f(inputs["ada_b"][:NL].reshape(NL, 48, 128).transpose(2, 0, 1).reshape(128, NL * 48))
    d["gmix"] = f(inputs["norm_mix_g"][:NL].reshape(NL, 8, 128).transpose(2, 0, 1).reshape(128, NL * 8))
    d["gmlp"] = f(inputs["norm_mlp_g"][:NL].reshape(NL, 8, 128).transpose(2, 0, 1).reshape(128, NL * 8))
    d["gfin"] = f(inputs["final_norm_g"].reshape(8, 128).T)
    d["w_in"] = f(inputs["w_in"][:NL])
    d["conv_w"] = f(inputs["conv_w"][:NL].reshape(NL, 3, 2, 128).transpose(3, 0, 2, 1).reshape(128, NL * 6))
    d["conv_b"] = f(inputs["conv_b"][:NL].reshape(NL, 2, 128).transpose(2, 0, 1).reshape(128, NL * 2))
    d["gng"] = f(np.broadcast_to(inputs["gmlp_norm_g"][:NL].reshape(1, NL * 256), (128, NL * 256)))
    d["swT"] = f(inputs["spatial_w"][:NL].transpose(3, 0, 1, 2).reshape(128, NL * 512))
    d["sb"] = f(inputs["spatial_b"][:NL].reshape(1, NL * 512))
    d["w_out"] = f(inputs["w_out"][:NL])
    d["w1"] = f(inputs["mlp_w1"][:NL])
    d["w2"] = f(inputs["mlp_w2"][:NL])
    return d


_NC_CACHE = {}


def kernel(**inputs):
    x = np.asarray(inputs["x"])
    B, S, _ = x.shape
    NL = int(np.asarray(inputs["ada_w"]).shape[0])
    inputs = {k: np.asarray(v) for k, v in inputs.items()}
    key = (S, NL)
    if key not in _NC_CACHE:
        _NC_CACHE[key] = build(S, NL)[0]
    nc = _NC_CACHE[key]
    shared = _layout(inputs, 0, NL)
    in_maps = []
    for b in range(B):
        d = dict(shared)
        d["x"] = np.ascontiguousarray(inputs["x"][b], dtype=np.float32)
        d["c"] = np.ascontiguousarray(inputs["c"][b].reshape(8, 128).T, dtype=np.float32)
        in_maps.append(d)
    res = run_bass_kernel_spmd(nc, in_maps, core_ids=list(range(B)))
    out = np.stack([np.asarray(r["out"]) for r in res.results], axis=0)
    return out.astype(np.float32)
```

```python
import contextlib
from collections import deque
import numpy as np
import concourse.bass as bass
import concourse.mybir as mybir
from concourse.bass_utils import run_bass_kernel_spmd

F32 = mybir.dt.float32
BF16 = mybir.dt.bfloat16
AF = mybir.ActivationFunctionType
ALU = mybir.AluOpType
AX = mybir.AxisListType

ENGS = ("pe", "act", "dve", "pool", "sp")
EPS = 1e-6


def I(method, *args, **kw):
    return lambda e: getattr(e, method)(*args, **kw)


class Op:
    __slots__ = ("eng", "fn", "reads", "writes", "dma", "deps", "signal", "sig", "idx", "eidx", "dsem", "dval",
                 "dsem_i")

    def __init__(self, eng, fn, reads, writes, dma):
        self.eng, self.fn, self.reads, self.writes, self.dma = eng, fn, reads, writes, dma
        self.deps = []
        self.signal = False
        self.sig = None
        self.dsem = None
        self.dval = None


class Sched:
    def __init__(self, nc, n_dma_sems=8, self_dist=10 ** 9):
        self.nc = nc
        self.ops = []
        self.lastw = {}
        self.readers = {}
        self.ecount = {e: 0 for e in ENGS}
        self.n_dma_sems = n_dma_sems
        self.self_dist = self_dist

    def op(self, eng, fn, reads=(), writes=(), dma=False):
        o = Op(eng, fn, tuple(reads), tuple(writes), dma)
        o.idx = len(self.ops)
        o.eidx = self.ecount[eng]
        self.ecount[eng] += 1
        deps = set()
        for t in o.reads:
            w = self.lastw.get(t)
            if w is not None:
                deps.add(w)
        for t in o.writes:
            w = self.lastw.get(t)
            if w is not None:
                deps.add(w)
            rd = self.readers.get(t)
            if rd:
                for v in rd[0].values():
                    deps.add(v)
                for v in rd[1]:
                    deps.add(v)
        deps.discard(o.idx)
        o.deps = sorted(deps)
        for t in o.reads:
            rd = self.readers.get(t)
            if rd is None:
                rd = self.readers[t] = ({}, [])
            if dma:
                rd[1].append(o.idx)
            else:
                rd[0][eng] = o.idx
        for t in o.writes:
            self.lastw[t] = o.idx
            self.readers[t] = ({}, [])
        self.ops.append(o)
        return o

    def pe(self, fn, reads=(), writes=()):
        return self.op("pe", fn, reads, writes)

    def act(self, fn, reads=(), writes=()):
        return self.op("act", fn, reads, writes)

    def dve(self, fn, reads=(), writes=()):
        return self.op("dve", fn, reads, writes)

    def pool(self, fn, reads=(), writes=()):
        return self.op("pool", fn, reads, writes)

    def dma(self, q, fn, reads=(), writes=()):
        return self.op(q, fn, reads, writes, dma=True)

    def _needs_sem(self, a, b):
        if a.dma:
            return True
        if a.eng != b.eng:
            return True
        if b.dma:
            return True
        if a.eng == "pe":
            return False
        return (b.eidx - a.eidx) <= self.self_dist

    def emit(self):
        nc = self.nc
        ops = self.ops
        for b in ops:
            for ai in b.deps:
                a = ops[ai]
                if not a.dma and self._needs_sem(a, b):
                    a.signal = True
        with contextlib.ExitStack() as st:
            csem = {e: st.enter_context(nc.semaphore("c_" + e)) for e in ("pe", "act", "dve", "pool")}
            dsems = {}
            for q in ("sp", "pool"):
                dsems[q] = [st.enter_context(nc.semaphore("d_%s%d" % (q, i))) for i in range(self.n_dma_sems)]
            dcnt = {q: [0] * self.n_dma_sems for q in dsems}
            dnext = {q: 0 for q in dsems}
            ccount = {e: 0 for e in csem}
            waited = {e: {} for e in ENGS}
            streams = {e: [] for e in ENGS}
            for o in ops:
                waits = []
                W = waited[o.eng]

                def need(sem, val, key):
                    if W.get(key, 0) < val:
                        W[key] = val
                        waits.append((sem, val))

                for ai in o.deps:
                    a = ops[ai]
                    if a.dma:
                        need(a.dsem, a.dval, ("d", a.eng, a.dsem_i))
                    elif a.signal and self._needs_sem(a, o):
                        need(csem[a.eng], a.sig, ("c", a.eng))
                if o.dma:
                    q = o.eng
                    i = dnext[q]
                    dnext[q] = (i + 1) % self.n_dma_sems
                    sem = dsems[q][i]
                    if dcnt[q][i] > 0:
                        need(sem, dcnt[q][i], ("d", q, i))
                    dcnt[q][i] += 16
                    o.dsem, o.dval = sem, dcnt[q][i]
                    o.dsem_i = i
                    streams[o.eng].append((waits, o.fn, sem, 16))
                else:
                    if o.signal:
                        ccount[o.eng] += 1
                        o.sig = ccount[o.eng]
                        streams[o.eng].append((waits, o.fn, csem[o.eng], 1))
                    else:
                        streams[o.eng].append((waits, o.fn, None, 0))
            finals = []
            for q in dsems:
                for i, sem in enumerate(dsems[q]):
                    if dcnt[q][i] > 0:
                        finals.append((sem, dcnt[q][i]))
            for e in csem:
                if ccount[e] > 0:
                    finals.append((csem[e], ccount[e]))
            self.stats = {e: len(streams[e]) for e in ENGS}
            self.stats["sig"] = dict(ccount)

            def run(engine, lst, extra=None):
                for waits, fn, sem, inc in lst:
                    for (s, v) in waits:
                        engine.wait_ge(s, v)
                    ins = fn(engine)
                    if sem is not None:
                        ins.then_inc(sem, inc)
                if extra:
                    for (s, v) in extra:
                        engine.wait_ge(s, v)

            with nc.Block() as block:
                @block.tensor
                def _(e):
                    run(e, streams["pe"])

                @block.scalar
                def _(e):
                    run(e, streams["act"])

                @block.vector
                def _(e):
                    run(e, streams["dve"])

                @block.gpsimd
                def _(e):
                    run(e, streams["pool"])

                @block.sync
                def _(e):
                    run(e, streams["sp"], extra=finals)


WIN_ORDER = [10, 9, 7, 8, 6, 4, 5, 2, 3, 0, 1]


def build(S, NL, dbg=None):
    NG = S // 512
    NT = S // 128
    nc = bass.Bass("TRN2", target_bir_lowering=False)

    def din(name, shape):
        return nc.dram_tensor(name, shape, F32, kind="ExternalInput").ap()

    x_d = din("x", [S, 1024])
    c_d = din("c", [128, 8])
    adaw_d = din("ada_w", [NL, 1024, 6144])
    adab_d = din("ada_b", [128, NL * 48])
    gmix_d = din("gmix", [128, NL * 8])
    gmlp_d = din("gmlp", [128, NL * 8])
    gfin_d = din("gfin", [128, 8])
    win_d = din("w_in", [NL, 1024, 2816])
    cw_d = din("conv_w", [128, NL * 6])
    cb_d = din("conv_b", [128, NL * 2])
    gng_d = din("gng", [128, NL * 256])
    swT_d = din("swT", [128, NL * 512])
    sb_d = din("sb", [1, NL * 512])
    wout_d = din("w_out", [NL, 1024, 1024])
    w1_d = din("w1", [NL, 1024, 4096])
    w2_d = din("w2", [NL, 4096, 1024])
    out_d = nc.dram_tensor("out", [S, 1024], F32, kind="ExternalOutput").ap()
    dbg_d = None
    if dbg:
        dbg_d = nc.dram_tensor("dbg", [128, 8 * S], F32, kind="ExternalOutput").ap()

    adaw_v = [adaw_d[l].rearrange("(k p) c -> p k c", p=128) for l in range(NL)]
    win_v = [win_d[l].rearrange("(k p) c -> p k c", p=128) for l in range(NL)]
    wout_v = [wout_d[l].rearrange("(k p) c -> p k c", p=128) for l in range(NL)]
    w1_v = [w1_d[l].rearrange("(k p) c -> p k c", p=128) for l in range(NL)]
    w2_v = [w2_d[l].rearrange("(k p) c -> p k c", p=128) for l in range(NL)]
    x_v = x_d.rearrange("(t p) d -> p t d", p=128)
    out_v = out_d.rearrange("(t p) d -> p t d", p=128)

    Hcells = max(8 * NG, 32)
    Qcells = max(12 * NG, 4 * NG + 32)
    if NG == 1:
        Qcells = 36
    off = [0]

    def carve(nbytes):
        o = off[0]
        off[0] += (nbytes + 63) // 64 * 64
        return o

    o_x = carve(8 * S * 4)
    o_H = carve(Hcells * 1024)
    o_Q = carve(Qcells * 1024)
    o_M = carve(8 * S)
    o_S = carve(16 * 1024)
    o_W = carve(16 * 1024)
    o_identf = carve(512)
    o_ctmp = carve(512)
    o_identb = carve(256)
    o_negtri = carve(256)
    o_negones = carve(256)
    o_ones = carve(256)
    o_negm = carve(256)
    o_c = carve(32)
    o_cact = carve(16)
    o_mod = carve(NL * 48 * 4)
    o_adab = carve(NL * 48 * 4)
    o_gmix = carve(NL * 8 * 4)
    o_gmlp = carve(NL * 8 * 4)
    o_gfin = carve(32)
    o_gsm = carve(NL * 8 * 4)
    o_gsf = carve(NL * 8 * 4)
    o_cw = carve(NL * 6 * 4)
    o_cb = carve(NL * 2 * 4)
    o_gng = carve(1024)
    o_gng16 = carve(1024)
    o_swT = carve(1024)
    o_sbb = carve(1024)
    o_modrow = carve(2048)
    o_vstat = carve(256)
    TOTAL = off[0]
    assert TOTAL <= 206 * 1024, TOTAL

    st = contextlib.ExitStack()
    arena = st.enter_context(nc.sbuf_tensor("arena", [128, TOTAL // 2], BF16))
    ps = st.enter_context(nc.psum_tensor("ps", [128, 8, 512], F32))

    def V(o, shape, dt):
        esz = 4 if dt == F32 else 2
        n = int(np.prod(shape))
        ap = arena[:, o // 2: o // 2 + n * esz // 2]
        if dt == F32:
            ap = ap.bitcast(F32)
        if len(shape) == 2:
            ap = ap.rearrange("p (a b) -> p a b", b=shape[1])
        elif len(shape) == 3:
            ap = ap.rearrange("p (a b c) -> p a b c", b=shape[1], c=shape[2])
        return ap

    Hc = lambda c: ("H", c)
    Qc = lambda c: ("Q", c)
    Sc = lambda c: ("S", c)
    Wc = lambda c: ("W", c)

    xT = V(o_x, [8, S], F32)
    hT = V(o_H, [8, S], BF16)
    wo = V(o_W, [8, 1024], BF16)
    Et = [V(o_H + i * 4096, [2, 512], F32) for i in range(2)]
    LT = [V(o_H + (8 + 2 * i) * 1024, [2, 512], BF16) for i in range(3)]
    WT = [V(o_H + (14 + 2 * i) * 1024, [2, 512], BF16) for i in range(3)]
    Rt = V(o_H + 20 * 1024, [2, 512], BF16)
    qT = [V(o_Q + (j * NG) * 1024, [S], BF16) for j in range(4)]
    kT = [V(o_Q + (4 * NG + j * NG) * 1024, [S], BF16) for j in range(4)]
    vv = V(o_Q + 8 * NG * 1024, [NT, 512], BF16)
    h1 = [V(o_Q + sub * NG * 1024, [S], BF16) for sub in range(4)]
    w1s = [V(o_W, [8, 512], BF16), V(o_Q + 4 * NG * 1024, [8, 512], BF16)]
    w2s = [V(o_W + 8192, [4, 1024], BF16), V(o_Q + (4 * NG + 8) * 1024, [4, 1024], BF16)]
    w1tok = [[Wc(0), Wc(1)], [Qc(4 * NG + i) for i in range(8)]]
    w2tok = [[Wc(2), Wc(3)], [Qc(4 * NG + 8 + i) for i in range(8)]]
    ADA0 = 0
    ADA1 = 4 * NG + 16
    XS0 = 16
    xs = [V(o_Q + (XS0 + i * 16) * 1024, [4, 1024], F32) for i in range(2 if NG > 1 else 1)]
    OF0 = 4 * NG + 16
    OS0 = 4 * NG
    of = V(o_Q + OF0 * 1024, [8, 512], F32)
    ost = [V(o_Q + (OS0 + 4 * i) * 1024, [1024], F32) for i in range(2)]
    cout = [V(o_M + j * S * 2, [S], BF16) for j in range(2)]
    sout = [V(o_M + 2 * S * 2 + j * S * 2, [S], BF16) for j in range(2)]
    gvb = V(o_S, [NT, 256], BF16)
    sqv = V(o_S + 8 * 1024, [256], F32)
    sq = [V(o_S + i * 1024, [512], BF16) for i in range(4)]
    rstds = [V(o_S + 4096 + i * 2048, [512], F32) for i in range(2)]
    tmp = [V(o_S + 8192 + i * 2048, [512], F32) for i in range(4)]
    Cc = V(o_S, [S], BF16)
    uu = V(o_S + 4096, [S + 2], F32)
    ytmp = V(o_S + 13 * 1024, [512], F32)
    wt = [V(o_W + i * 4096, [8, 256], BF16) for i in range(4)]
    ident_f = V(o_identf, [128], F32)
    ctmp = V(o_ctmp, [128], F32)
    ident_b = V(o_identb, [128], BF16)
    negtri = V(o_negtri, [128], BF16)
    negones = V(o_negones, [128], BF16)
    ones_b = V(o_ones, [128], BF16)
    negm = V(o_negm, [128], BF16)
    c_sb = V(o_c, [8], F32)
    cact = V(o_cact, [8], BF16)
    mod = V(o_mod, [NL * 48], F32)
    adab = V(o_adab, [NL * 48], F32)
    gmix = V(o_gmix, [NL * 8], F32)
    gmlp = V(o_gmlp, [NL * 8], F32)
    gfin = V(o_gfin, [8], F32)
    gsm = V(o_gsm, [NL * 8], F32)
    gsf = V(o_gsf, [NL * 8], F32)
    cw = V(o_cw, [NL * 6], F32)
    cb = V(o_cb, [NL * 2], F32)
    gng = V(o_gng, [256], F32)
    gng16 = V(o_gng16, [256], F32)
    swT = V(o_swT, [4, 128], BF16)
    sbb = V(o_sbb, [512], BF16)
    modrow = [V(o_modrow, [512], F32)]
    vstat = V(o_vstat, [64], F32)

    sc = Sched(nc)
    bank_rr = [0]

    def nb():
        b = bank_rr[0]
        bank_rr[0] = (b + 1) % 8
        return b

    bg = deque()

    def bg_step(n=1):
        for _ in range(n):
            if bg:
                bg.popleft()()

    sc.pool(I("memset", ident_f, 1.0), writes=["identf"])
    sc.pool(I("affine_select", out=ident_f, in_=ident_f, pattern=[[-1, 128]], compare_op=ALU.is_equal,
                                      fill=0.0, base=0, channel_multiplier=1), reads=["identf"], writes=["identf"])
    sc.dve(I("tensor_copy", out=ident_b, in_=ident_f), reads=["identf"], writes=["identb"])
    sc.pool(I("memset", ctmp, -1.0), writes=["ctmp"])
    sc.pool(I("affine_select", out=ctmp, in_=ctmp, pattern=[[-1, 128]], compare_op=ALU.is_ge,
                                      fill=0.0, base=0, channel_multiplier=1), reads=["ctmp"], writes=["ctmp"])
    sc.dve(I("tensor_copy", out=negtri, in_=ctmp), reads=["ctmp"], writes=["negtri"])
    sc.pool(I("memset", ctmp, -30000.0), reads=["ctmp"], writes=["ctmp"])
    sc.pool(I("affine_select", out=ctmp, in_=ctmp, pattern=[[-1, 128]], compare_op=ALU.is_ge,
                                      fill=0.0, base=0, channel_multiplier=1), reads=["ctmp"], writes=["ctmp"])
    sc.dve(I("tensor_copy", out=negm, in_=ctmp), reads=["ctmp"], writes=["negm"])
    sc.pool(I("memset", negones, -1.0), writes=["negones"])
    sc.pool(I("memset", ones_b, 1.0), writes=["ones"])
    for (dst, src, tok) in ((c_sb, c_d, "c"), (adab, adab_d, "adab"), (gmix, gmix_d, "gmix"), (gmlp, gmlp_d, "gmlp"),
                            (gfin, gfin_d, "gfin"), (cw, cw_d, "cw"), (cb, cb_d, "cb")):
        sc.dma("sp", I("dma_start", out=dst, in_=src), writes=[tok])
    sc.act(I("activation", out=cact, in_=c_sb, func=AF.Silu), reads=["c"], writes=["cact"])

    def ada_setup(l, base):
        tiles = [V(o_Q + (base + 8 * i) * 1024, [8, 512], BF16) for i in range(2)]
        ttok = [[Qc(base + 8 * i + c) for c in range(8)] for i in range(2)]

        def issue(ct):
            sc.dma("pool", I("dma_start", out=tiles[ct % 2], in_=adaw_v[l][:, :, ct * 512:(ct + 1) * 512]),
                   writes=ttok[ct % 2])

        def item(ct):
            def f():
                if ct + 1 < 12:
                    issue(ct + 1)
                b = nb()
                for k in range(8):
                    sc.pe(I("matmul", ps[0:1, b, :], lhsT=cact[:, k:k + 1], rhs=tiles[ct % 2][:, k, :],
                                                  start=(k == 0), stop=(k == 7)),
                          reads=ttok[ct % 2] + ["cact"], writes=[("ps", b)])
                mr = modrow[0]
                sc.dve(I("tensor_copy", out=mr[0:1, :], in_=ps[0:1, b, :]), reads=[("ps", b)],
                       writes=[("modrow", 0)])
                b2 = nb()
                for jj in range(4):
                    sc.pe(I("matmul", ps[:, b2, jj:jj + 1], lhsT=mr[0:1, jj * 128:(jj + 1) * 128],
                                                    rhs=ident_f[0:1, 0:1], start=True, stop=True),
                          reads=[("modrow", 0), "identf"], writes=[("ps", b2)])
                c0 = l * 48 + ct * 4
                sc.dve(I("tensor_tensor", out=mod[:, c0:c0 + 4], in0=ps[:, b2, 0:4], in1=adab[:, c0:c0 + 4],
                                                 op=ALU.add), reads=[("ps", b2), "adab"], writes=[("mod", l, ct)])
                if ct == 3:
                    sc.dve(I("scalar_tensor_tensor", out=gsm[:, l * 8:l * 8 + 8], in0=mod[:, l * 48 + 8:l * 48 + 16],
                                                            scalar=1.0, in1=gmix[:, l * 8:l * 8 + 8], op0=ALU.add,
                                                            op1=ALU.mult),
                           reads=[("mod", l, 2), ("mod", l, 3), "gmix"], writes=[("gsm", l)])
                if ct == 9:
                    sc.dve(I("scalar_tensor_tensor", out=gsf[:, l * 8:l * 8 + 8], in0=mod[:, l * 48 + 32:l * 48 + 40],
                                                            scalar=1.0, in1=gmlp[:, l * 8:l * 8 + 8], op0=ALU.add,
                                                            op1=ALU.mult),
                           reads=[("mod", l, 8), ("mod", l, 9), "gmlp"], writes=[("gsf", l)])
            return f

        issue(0)
        return [item(ct) for ct in range(12)]

    def modtok(l, c0):
        return [("mod", l, c0 // 4), ("mod", l, c0 // 4 + 1)]

    def emit_norm(gs_ap, sh_ap, gtoks):
        for g in range(NG):
            norm_group(g, gs_ap, sh_ap, gtoks)

    def norm_group(g, gs_ap, sh_ap, gtoks):
        b = nb()
        gsl = slice(g * 512, (g + 1) * 512)
        rstd = rstds[g % 2]
        rtk = [Sc(4 + 2 * (g % 2)), Sc(5 + 2 * (g % 2))]
        for k in range(8):
            sc.act(I("activation", out=sq[k % 4], in_=xT[:, k, gsl], func=AF.Square),
                   reads=[("x", k, g)], writes=[Sc(k % 4)])
            sc.pe(I("matmul", ps[:, b, :], lhsT=ones_b, rhs=sq[k % 4], start=(k == 0), stop=(k == 7)),
                  reads=[Sc(k % 4), "ones"], writes=[("ps", b)])
        sc.act(I("activation", out=rstd, in_=ps[:, b, :], func=AF.Ln, scale=1.0 / 1024, bias=EPS),
               reads=[("ps", b)], writes=rtk)
        sc.act(I("activation", out=rstd, in_=rstd, func=AF.Exp, scale=-0.5), reads=rtk, writes=rtk)
        for k in range(8):
            tt = tmp[k % 4]
            ttk = [Sc(8 + 2 * (k % 4)), Sc(9 + 2 * (k % 4))]
            sc.dve(I("scalar_tensor_tensor", out=tt, in0=xT[:, k, gsl], scalar=gs_ap[:, k:k + 1], in1=rstd,
                     op0=ALU.mult, op1=ALU.mult), reads=[("x", k, g)] + rtk + gtoks, writes=ttk)
            if k % 2 == 0:
                sc.act(I("activation", out=hT[:, k, gsl], in_=tt, func=AF.Identity, bias=sh_ap[:, k:k + 1], scale=1.0),
                       reads=ttk + gtoks, writes=[Hc(k * NG + g)])
            else:
                sc.dve(I("tensor_scalar", out=hT[:, k, gsl], in0=tt, scalar1=sh_ap[:, k:k + 1], scalar2=None,
                         op0=ALU.add), reads=ttk + gtoks, writes=[Hc(k * NG + g)])

    def win_issue(l, n):
        t = WIN_ORDER[n]
        slot = n % 4
        sc.dma("pool", I("dma_start", out=wt[slot], in_=win_v[l][:, :, t * 256:(t + 1) * 256]),
               writes=[Wc(slot)])

    def fm_group(slot, sub, g, evac):
        b = nb()
        for k in range(8):
            sc.pe(I("matmul", ps[:, b, :], lhsT=wt[slot][:, k, sub * 128:(sub + 1) * 128],
                                          rhs=hT[:, k, g * 512:(g + 1) * 512], start=(k == 0), stop=(k == 7)),
                  reads=[Wc(slot), Hc(k * NG + g)], writes=[("ps", b)])
        evac(b)

    def tm_group(slot, tt, evac):
        b = nb()
        for k in range(8):
            sc.pe(I("matmul", ps[:, b, 0:256], lhsT=hT[:, k, tt * 128:(tt + 1) * 128],
                                          rhs=wt[slot][:, k, :], start=(k == 0), stop=(k == 7)),
                  reads=[Wc(slot), Hc(k * NG + tt // 4)], writes=[("ps", b)])
        evac(b)

    def emit_proj(l, norm_hook=None):
        sc.dma("sp", I("dma_start", out=gng, in_=gng_d[:, l * 256:(l + 1) * 256]), writes=["gng"])
        sc.dma("pool", I("dma_start", out=swT, in_=swT_d[:, l * 512:(l + 1) * 512].rearrange("p (a b) -> p a b", b=128)),
               writes=["swT"])
        sc.pool(I("memset", swT[64:128, :, 0:64], 0.0), reads=["swT"], writes=["swT"])
        sc.dma("pool", I("dma_start", out=sbb[0:1, :], in_=sb_d[0:1, l * 512:(l + 1) * 512]), writes=["sbb"])
        for n in range(11):
            t = WIN_ORDER[n]
            slot = n % 4
            if t == 9:
                for sub in range(2):
                    for g in range(NG):
                        def ev(b, sub=sub, g=g):
                            sc.act(I("activation", out=sout[sub][:, g * 512:(g + 1) * 512], in_=ps[:, b, :],
                                     func=AF.Gelu_apprx_tanh), reads=[("ps", b)], writes=[("so", sub, g)])
                        fm_group(slot, sub, g, ev)
                vst = [("vs", tt) for tt in range(NT)]
                sc.act(I("activation", out=vstat[:, NT:2 * NT], in_=vstat[:, 0:NT], func=AF.Ln, scale=1.0 / 256, bias=EPS),
                       reads=vst, writes=["vs2"])
                sc.act(I("activation", out=vstat[:, NT:2 * NT], in_=vstat[:, NT:2 * NT], func=AF.Exp, scale=-0.5),
                       reads=["vs2"], writes=["vs2"])
                def scale_tile(tt):
                    sc.dve(I("scalar_tensor_tensor", out=gvb[:, tt, :], in0=gvb[:, tt, :], scalar=vstat[:, NT + tt:NT + tt + 1],
                             in1=gng16, op0=ALU.mult, op1=ALU.mult),
                           reads=[("gvb", tt), Sc(tt // 2), "vs2", "gng16"], writes=[("gvb", tt)])

                def mm_tile(tt):
                    b2 = nb()
                    for hh in range(4):
                        j = hh // 2
                        sc.pe(I("matmul", ps[:, b2, hh * 128:(hh + 1) * 128], lhsT=gvb[:, tt, j * 128:(j + 1) * 128],
                                rhs=swT[:, hh, :], start=True, stop=False),
                              reads=[("gvb", tt), Sc(tt // 2), "swT"], writes=[("ps", b2)])
                        sc.pe(I("matmul", ps[:, b2, hh * 128:(hh + 1) * 128], lhsT=ones_b[0:1, :],
                                rhs=sbb[0:1, hh * 128:(hh + 1) * 128], start=False, stop=True),
                              reads=["sbb", "ones"], writes=[("ps", b2)])
                    return b2

                def evac_tile(tt, b2):
                    g = tt // 4
                    tsl = slice(tt * 128, (tt + 1) * 128)
                    for hh in range(4):
                        j = hh // 2
                        rb = 64 * (hh % 2)
                        sc.dve(I("tensor_tensor", out=sout[j][rb:rb + 64, tsl], in0=sout[j][rb:rb + 64, tsl],
                                 in1=ps[rb:rb + 64, b2, hh * 128:(hh + 1) * 128], op=ALU.mult),
                               reads=[("ps", b2), ("so", j, g)], writes=[("so", j, g)])

                scale_tile(0)
                if NT > 1:
                    scale_tile(1)
                pend = None
                for tt in range(NT):
                    b2 = mm_tile(tt)
                    if tt + 2 < NT:
                        scale_tile(tt + 2)
                    if pend is not None:
                        evac_tile(*pend)
                    pend = (tt, b2)
                evac_tile(*pend)
            elif t == 10:
                sc.dve(I("tensor_scalar", out=gng16, in0=gng, scalar1=1.0, scalar2=None, op0=ALU.mult),
                       reads=["gng"], writes=["gng16"])
                for tt in range(NT):
                    if norm_hook is not None:
                        norm_hook(tt)

                    def evA(b, tt=tt):
                        sc.act(I("activation", out=gvb[:, tt, :], in_=ps[:, b, 0:256], func=AF.Gelu_apprx_tanh),
                               reads=[("ps", b)], writes=[Sc(tt // 2), ("gvb", tt)])
                        sc.dve(I("tensor_tensor", out=sqv, in0=gvb[:, tt, :], in1=gvb[:, tt, :], op=ALU.mult),
                               reads=[("gvb", tt), Sc(tt // 2)], writes=[Sc(8)])
                        sc.dve(I("reduce_sum", out=vstat[:, tt:tt + 1], in_=sqv, axis=AX.X),
                               reads=[Sc(8)], writes=[("vs", tt)])
                    tm_group(slot, tt, evA)
            elif t == 7:
                pass
            elif t == 8:
                pass
            elif t == 6:
                sC, sH, sB = (n - 2) % 4, (n - 1) % 4, n % 4
                for j in range(2):
                    for g in range(NG):
                        def evC(b, g=g):
                            sc.act(I("activation", out=Cc[:, g * 512:(g + 1) * 512], in_=ps[:, b, :], func=AF.Copy),
                                   reads=[("ps", b)], writes=[Sc(g)])
                        fm_group(sC, j, g, evC)
                    sc.dve(I("memset", uu[:, 0:2], 0.0), writes=[Sc(4)])
                    for g in range(NG):
                        def evH(b, g=g):
                            utok = [Sc(4 + 2 * g + i) for i in range(3)]
                            sc.dve(I("tensor_tensor", out=uu[:, 2 + g * 512:2 + (g + 1) * 512], in0=ps[:, b, :],
                                                             in1=Cc[:, g * 512:(g + 1) * 512], op=ALU.mult),
                                   reads=[("ps", b), Sc(g)], writes=utok)
                        fm_group(sH, j, g, evH)
                    for g in range(NG):
                        def evB(b, g=g, j=j):
                            utok = [Sc(4 + 2 * g + i) for i in range(-2, 3) if 4 + 2 * g + i >= 4]
                            c6 = l * 6 + j * 3
                            ytok = [Sc(13), Sc(14)]
                            sc.dve(I("tensor_scalar", out=ytmp, in0=uu[:, 2 + g * 512:2 + (g + 1) * 512],
                                                             scalar1=cw[:, c6 + 2:c6 + 3], scalar2=None, op0=ALU.mult),
                                   reads=utok + ["cw"], writes=ytok)
                            sc.dve(I("scalar_tensor_tensor", out=ytmp, in0=uu[:, 1 + g * 512:1 + (g + 1) * 512],
                                                                    scalar=cw[:, c6 + 1:c6 + 2], in1=ytmp, op0=ALU.mult,
                                                                    op1=ALU.add),
                                   reads=utok + ytok, writes=ytok)
                            sc.dve(I("scalar_tensor_tensor", out=ytmp, in0=uu[:, g * 512:(g + 1) * 512],
                                                                    scalar=cw[:, c6:c6 + 1], in1=ytmp, op0=ALU.mult,
                                                                    op1=ALU.add),
                                   reads=utok + ytok, writes=ytok)
                            sc.dve(I("scalar_tensor_tensor", out=cout[j][:, g * 512:(g + 1) * 512], in0=ytmp,
                                                                    scalar=cb[:, l * 2 + j:l * 2 + j + 1], in1=ps[:, b, :],
                                                                    op0=ALU.add, op1=ALU.mult),
                                   reads=ytok + [("ps", b), "cb"], writes=[("co", j, g)])
                        fm_group(sB, j, g, evB)
            elif t in (4, 5):
                half = t - 4
                for tt in range(NT):
                    def ev(b, tt=tt):
                        sc.dve(I("tensor_copy", out=vv[:, tt, half * 256:(half + 1) * 256], in_=ps[:, b, 0:256]),
                               reads=[("ps", b)], writes=[Qc(8 * NG + tt)])
                    tm_group(slot, tt, ev)
            elif t in (2, 3):
                for sub in range(2):
                    j = (t - 2) * 2 + sub
                    for g in range(NG):
                        def ev(b, j=j, g=g):
                            sc.act(I("activation", out=kT[j][:, g * 512:(g + 1) * 512], in_=ps[:, b, :], func=AF.Copy),
                                   reads=[("ps", b)], writes=[Qc(4 * NG + j * NG + g)])
                        fm_group(slot, sub, g, ev)
            else:
                if t == 0 and n == 9:
                    pass
                for sub in range(2):
                    j = t * 2 + sub
                    for g in range(NG):
                        def ev(b, j=j, g=g):
                            sc.dve(I("tensor_scalar", out=qT[j][:, g * 512:(g + 1) * 512], in0=ps[:, b, :],
                                                             scalar1=0.125, scalar2=None, op0=ALU.mult),
                                   reads=[("ps", b)], writes=[Qc(j * NG + g)])
                        fm_group(slot, sub, g, ev)
            if t in (7, 8):
                continue
            if t == 6:
                for nn in (n - 2, n - 1, n):
                    if nn + 4 < 11:
                        win_issue(l, nn + 4)
            elif n + 4 < 11:
                win_issue(l, n + 4)
            if n == 8:
                bg_step(len(bg))
            else:
                bg_step(2 if n == 0 else 1)

    def emit_attn(l):
        items = []
        for j in range(4):
            for m in range(NG):
                for c in range(4 * m + 3, -1, -1):
                    items.append((j, m, c))
        n = len(items)

        def geom(it):
            j, m, c = it
            r = c - 4 * m
            diag = r >= 0
            v0 = 128 * r if diag else 0
            return j, m, c, diag, v0

        def ztok(i):
            return [("ps", 2 * (i % 3)), ("ps", 2 * (i % 3) + 1)]

        def stageA(i):
            j, m, c, diag, v0 = geom(items[i])
            t0 = m * 512
            ktok = Qc(4 * NG + j * NG + c // 4)
            qtok = Qc(j * NG + m)
            for hp in range(2):
                rb = 64 * hp
                zb = 2 * (i % 3) + hp
                kk = kT[j][rb:rb + 64, c * 128:(c + 1) * 128]
                if diag:
                    rest = v0 + 128 < 512
                    if rest:
                        sc.pe(I("matmul", ps[:, zb, v0 + 128:512], lhsT=kk, rhs=qT[j][rb:rb + 64, t0 + v0 + 128:t0 + 512],
                                start=True, stop=False, skip_group_check=True), reads=[ktok, qtok], writes=[("ps", zb)])
                    sc.pe(I("matmul", ps[:, zb, v0:v0 + 128], lhsT=kk, rhs=qT[j][rb:rb + 64, t0 + v0:t0 + v0 + 128],
                            start=not rest, stop=False, skip_group_check=True), reads=[ktok, qtok], writes=[("ps", zb)])
                    sc.pe(I("matmul", ps[:, zb, v0:v0 + 128], lhsT=ident_b, rhs=negm, start=False, stop=True,
                            skip_group_check=True), reads=["identb", "negm"], writes=[("ps", zb)])
                else:
                    sc.pe(I("matmul", ps[:, zb, :], lhsT=kk, rhs=qT[j][rb:rb + 64, t0:t0 + 512], start=True, stop=True,
                            skip_group_check=True), reads=[ktok, qtok], writes=[("ps", zb)])

        def stageB(i):
            j, m, c, diag, v0 = geom(items[i])
            z0 = 2 * (i % 3)
            E = Et[i % 2]
            etok = [Hc(4 * (i % 2) + q) for q in range(4)]
            L = LT[i % 3]
            ltok = [Hc(8 + 2 * (i % 3)), Hc(9 + 2 * (i % 3))]
            sc.act(I("activation", out=E[:, :, v0:512], in_=ps[:, z0:z0 + 2, v0:512], func=AF.Exp),
                   reads=ztok(i), writes=etok)
            sc.act(I("activation", out=L[:, :, v0:512], in_=E[:, :, v0:512], func=AF.Ln, bias=1.0),
                   reads=etok, writes=ltok)

        def stageC(i):
            j, m, c, diag, v0 = geom(items[i])
            z0 = 2 * (i % 3)
            L = LT[i % 3]
            ltok = [Hc(8 + 2 * (i % 3)), Hc(9 + 2 * (i % 3))]
            wtok = [Hc(14 + 2 * (i % 3)), Hc(15 + 2 * (i % 3))]
            rtok = [Hc(20), Hc(21)]
            cmax = 4 * m + 3
            r0 = v0 + 128 if diag else 0
            hasR = (c < cmax) and r0 < 512
            for hp in range(2):
                zb = z0 + hp
                sc.pe(I("matmul", ps[:, zb, v0:512], lhsT=negtri, rhs=L[:, hp, v0:512], start=False, stop=not hasR,
                        skip_group_check=True), reads=ltok + ["negtri"], writes=[("ps", zb)])
                if hasR:
                    sc.pe(I("matmul", ps[:, zb, r0:512], lhsT=negones, rhs=Rt[:, hp, r0:512], start=False, stop=True,
                            skip_group_check=True), reads=rtok + ["negones"], writes=[("ps", zb)])
            sc.act(I("activation", out=WT[i % 3][:, :, v0:512], in_=ps[:, z0:z0 + 2, v0:512], func=AF.Exp),
                   reads=ztok(i), writes=wtok)
            if c > 0:
                if c == cmax:
                    sc.pool(I("memset", Rt, 0.0), writes=rtok)
                sc.pool(I("tensor_tensor", out=Rt[:, :, v0:512], in0=Rt[:, :, v0:512], in1=L[:, :, v0:512], op=ALU.add),
                        reads=rtok + ltok, writes=rtok)

        def stageE(i):
            j, m, c, diag, v0 = geom(items[i])
            W = WT[i % 3]
            wtok = [Hc(14 + 2 * (i % 3)), Hc(15 + 2 * (i % 3))]
            vtok = Qc(8 * NG + c)
            last = (c == 0)
            first = (c == 4 * m + 3)
            for hp in range(2):
                sc.pe(I("matmul", ps[64 * hp:64 * hp + 64, 6, v0:512], lhsT=vv[:, c, (2 * j + hp) * 64:(2 * j + hp + 1) * 64],
                        rhs=W[:, hp, v0:512], start=first, stop=last, skip_group_check=True),
                      reads=[vtok] + wtok, writes=[("ps", 6)])
            if last:
                sc.dve(I("tensor_copy", out=qT[j][:, m * 512:(m + 1) * 512], in_=ps[:, 6, :]),
                       reads=[("ps", 6)], writes=[Qc(j * NG + m)])
                for ct in range(8):
                    fillers.append((l, [j], m, ct))

        for g in range(NG):
            for ct in range(8):
                fillers.append((l, [4, 5], g, ct))
                fillers.append((l, [6, 7], g, ct))

        popped = [0]
        early = [False, False]
        for i in range(n + 2):
            if i < n:
                stageA(i)
                stageB(i)
            if i >= 4:
                pop_filler(1)
                popped[0] += 1
                if popped[0] == 16 * NG + 2 and not early[1]:
                    issue_w2(l, 0)
                    early[1] = True
            if NG >= 4 and i < n and items[i][0] == 2 and items[i][1] == 0 and items[i][2] == 3:
                issue_w1(l, 0)
                early[0] = True
            if 0 <= i - 1 < n:
                stageC(i - 1)
            if 0 <= i - 2 < n:
                stageE(i - 2)
        return early

    fillers = deque()

    def wout_partial(l, kts, g, ct, bank=7):
        mix = [qT[0], qT[1], qT[2], qT[3], cout[0], cout[1], sout[0], sout[1]]

        def mtok(kt):
            if kt < 4:
                return Qc(kt * NG + g)
            if kt < 6:
                return ("co", kt - 4, g)
            return ("so", kt - 6, g)
        for q, kt in enumerate(kts):
            sc.pe(I("matmul", ps[:, bank, :], lhsT=wo[:, kt, ct * 128:(ct + 1) * 128], rhs=mix[kt][:, g * 512:(g + 1) * 512],
                    start=(q == 0), stop=(q == len(kts) - 1)),
                  reads=[Wc(kt // 2), mtok(kt)], writes=[("ps", bank)])
        sc.dve(I("scalar_tensor_tensor", out=xT[:, ct, g * 512:(g + 1) * 512], in0=ps[:, bank, :],
                 scalar=mod[:, l * 48 + 16 + ct:l * 48 + 17 + ct], in1=xT[:, ct, g * 512:(g + 1) * 512],
                 op0=ALU.mult, op1=ALU.add),
               reads=[("ps", bank), ("x", ct, g)] + modtok(l, 16), writes=[("x", ct, g)])

    def pop_filler(k=1):
        for _ in range(k):
            if fillers:
                wout_partial(*fillers.popleft())

    def emit_wout(l):
        def nf(g):
            norm_group(g, gsf[:, l * 8:(l + 1) * 8], mod[:, l * 48 + 24:l * 48 + 32], [("gsf", l)] + modtok(l, 24))
        pending = sorted(set(f[2] for f in fillers))
        for g in range(NG):
            if g not in pending:
                nf(g)
        while fillers:
            wout_partial(*fillers.popleft(), bank=nb())
        for g in pending:
            nf(g)

    def issue_w1(l, ft):
        if ft < 8:
            s1 = (ft + 1) % 2
            sc.dma("pool", I("dma_start", out=w1s[s1], in_=w1_v[l][:, :, ft * 512:(ft + 1) * 512]), writes=w1tok[s1])

    def issue_w2(l, ft):
        if ft < 8:
            s2 = ft % 2
            sc.dma("pool", I("dma_start", out=w2s[s2], in_=w2_v[l][:, ft * 4:(ft + 1) * 4, :]), writes=w2tok[s2])

    def emit_ffn(l, tail):
        for ft in range(8):
            s1 = (ft + 1) % 2
            s2 = ft % 2
            issue_w2(l, ft + 1)
            order1 = [(sub, g) for sub in range(4) for g in range(NG)]
            if ft == 0:
                order1 = [(sub, g) for g in range(NG) for sub in range(4)]
            for (sub, g) in order1:
                b = nb()
                for k in range(8):
                    sc.pe(I("matmul", ps[:, b, :], lhsT=w1s[s1][:, k, sub * 128:(sub + 1) * 128],
                            rhs=hT[:, k, g * 512:(g + 1) * 512], start=(k == 0), stop=(k == 7)),
                          reads=w1tok[s1] + [Hc(k * NG + g)], writes=[("ps", b)])
                tq = tmp[b % 4]
                tqk = [Sc(8 + 2 * (b % 4)), Sc(9 + 2 * (b % 4))]
                sc.act(I("activation", out=tq, in_=ps[:, b, :], func=AF.Square), reads=[("ps", b)], writes=tqk)
                sc.dve(I("scalar_tensor_tensor", out=h1[sub][:, g * 512:(g + 1) * 512], in0=ps[:, b, :], scalar=0.0,
                         in1=tq, op0=ALU.is_gt, op1=ALU.mult), reads=[("ps", b)] + tqk, writes=[Qc(sub * NG + g)])
            bg_step(1)
            issue_w1(l, ft + 2)
            if ft == 7 and l + 1 < NL:
                for n in range(4):
                    win_issue(l + 1, n)
            order2 = [(ct, g) for ct in range(8) for g in range(NG)]
            if ft == 7:
                order2 = [(ct, g) for g in range(NG) for ct in range(8)]
            for (ct, g) in order2:
                b = nb()
                for kk in range(4):
                    sc.pe(I("matmul", ps[:, b, :], lhsT=w2s[s2][:, kk, ct * 128:(ct + 1) * 128],
                            rhs=h1[kk][:, g * 512:(g + 1) * 512], start=(kk == 0), stop=(kk == 3)),
                          reads=w2tok[s2] + [Qc(kk * NG + g)], writes=[("ps", b)])
                sc.dve(I("scalar_tensor_tensor", out=xT[:, ct, g * 512:(g + 1) * 512], in0=ps[:, b, :],
                         scalar=mod[:, l * 48 + 40 + ct:l * 48 + 41 + ct], in1=xT[:, ct, g * 512:(g + 1) * 512],
                         op0=ALU.mult, op1=ALU.add),
                       reads=[("ps", b), ("x", ct, g)] + modtok(l, 40), writes=[("x", ct, g)])
                if ft == 7 and ct == 7 and g >= 1:
                    tail(g - 1)
            if ft == 7:
                tail(NG - 1)
            bg_step(1)

    def final_group(g):
        if True:
            b = nb()
            gsl = slice(g * 512, (g + 1) * 512)
            rstd = rstds[g % 2]
            rtk = [Sc(4 + 2 * (g % 2)), Sc(5 + 2 * (g % 2))]
            for k in range(8):
                sc.act(I("activation", out=sq[k % 4], in_=xT[:, k, gsl], func=AF.Square),
                       reads=[("x", k, g)], writes=[Sc(k % 4)])
                sc.pe(I("matmul", ps[:, b, :], lhsT=ones_b, rhs=sq[k % 4], start=(k == 0), stop=(k == 7)),
                      reads=[Sc(k % 4), "ones"], writes=[("ps", b)])
            sc.act(I("activation", out=rstd, in_=ps[:, b, :], func=AF.Ln, scale=1.0 / 1024, bias=EPS),
                   reads=[("ps", b)], writes=rtk)
            sc.act(I("activation", out=rstd, in_=rstd, func=AF.Exp, scale=-0.5), reads=rtk, writes=rtk)
            for k in range(8):
                sc.dve(I("scalar_tensor_tensor", out=of[:, k, :], in0=xT[:, k, gsl], scalar=gfin[:, k:k + 1],
                         in1=rstd, op0=ALU.mult, op1=ALU.mult),
                       reads=[("x", k, g), "gfin"] + rtk, writes=[Qc(OF0 + 2 * k), Qc(OF0 + 2 * k + 1)])
            for i in range(4):
                tt = g * 4 + i
                oi = tt % 2
                otok = [Qc(OS0 + 4 * oi + c) for c in range(4)]
                for half in range(2):
                    b2 = nb()
                    for kq in range(4):
                        k = half * 4 + kq
                        sc.pe(I("transpose", ps[:, b2, kq * 128:(kq + 1) * 128],
                                                                of[:, k, i * 128:(i + 1) * 128], ident_f),
                              reads=[Qc(OF0 + 2 * k), Qc(OF0 + 2 * k + 1), "identf"], writes=[("ps", b2)])
                    if half == 0:
                        sc.act(I("activation", out=ost[oi][:, 0:512], in_=ps[:, b2, :], func=AF.Copy),
                               reads=[("ps", b2)], writes=otok)
                    else:
                        sc.dve(I("tensor_copy", out=ost[oi][:, 512:1024], in_=ps[:, b2, :]),
                               reads=[("ps", b2)], writes=otok)
                sc.dma("sp", I("dma_start", out=out_v[:, tt, :], in_=ost[oi]), reads=otok)

    def dump(items):
        o = 0
        for ap, toks, n in items:
            sc.dma("pool", I("dma_start", out=dbg_d[:, o:o + n], in_=ap), reads=toks)
            o += n

    def xitems():
        return [(xT[:, k, :], [("x", k, g) for g in range(NG)], S) for k in range(8)]

    ada0 = ada_setup(0, ADA0)
    n_up = 4 if NG >= 4 else 12

    def norm_m_group(l, g):
        norm_group(g, gsm[:, l * 8:(l + 1) * 8], mod[:, l * 48:l * 48 + 8], [("gsm", l)] + modtok(l, 0))

    def xload(g):
        xi = g % len(xs)
        xtok = [Qc(XS0 + xi * 16 + c) for c in range(16)]
        sc.dma("sp", I("dma_start", out=xs[xi], in_=x_v[:, g * 4:(g + 1) * 4, :]), writes=xtok)
        for k in range(8):
            b = nb()
            for i in range(4):
                sc.pe(I("transpose", ps[:, b, i * 128:(i + 1) * 128], xs[xi][:, i, k * 128:(k + 1) * 128], ident_f),
                      reads=xtok + ["identf"], writes=[("ps", b)])
            if k % 2 == 0:
                sc.act(I("activation", out=xT[:, k, g * 512:(g + 1) * 512], in_=ps[:, b, :], func=AF.Copy),
                       reads=[("ps", b)], writes=[("x", k, g)])
            else:
                sc.dve(I("tensor_copy", out=xT[:, k, g * 512:(g + 1) * 512], in_=ps[:, b, :]),
                       reads=[("ps", b)], writes=[("x", k, g)])

    up = list(ada0[:n_up])
    half = (len(up) + 1) // 2
    for g in range(NG):
        xload(g)
        if g == 0:
            for f in up[:half]:
                f()
        if g == min(1, NG - 1):
            for f in up[half:] if NG > 1 else up[half:]:
                f()
        if g >= 2:
            norm_m_group(0, g - 2)
    for f in ada0[n_up:]:
        bg.append(f)
    for n in range(4):
        win_issue(0, n)
    for g in range(max(NG - 2, 0), NG):
        norm_m_group(0, g)
    l0_hook = None
    if dbg == "x0":
        dump(xitems())
    for l in range(NL):
        if dbg == "h0" and l == 0:
            dump([(hT[:, k, :], [Hc(k * NG + g) for g in range(NG)], S) for k in range(8)])
        emit_proj(l, None)
        bg_step(len(bg))
        if dbg == "qk" and l == 0:
            dump([(qT[j], [Qc(j * NG + g) for g in range(NG)], S) for j in range(4)] +
                 [(kT[j], [Qc(4 * NG + j * NG + g) for g in range(NG)], S) for j in range(4)])
        if dbg == "v" and l == 0:
            dump([(vv[:, tt, :], [Qc(8 * NG + tt)], 512) for tt in range(NT)])
        if dbg == "cs" and l == 0:
            dump([(cout[j], [("co", j, g) for g in range(NG)], S) for j in range(2)] +
                 [(sout[j], [("so", j, g) for g in range(NG)], S) for j in range(2)])
        for q4 in range(4):
            sc.dma("pool", I("dma_start", out=wo[:, 2 * q4:2 * q4 + 2, :], in_=wout_v[l][:, 2 * q4:2 * q4 + 2, :]),
                   writes=[Wc(q4)])
        early = emit_attn(l)
        if not early[0]:
            issue_w1(l, 0)
        if dbg == "att" and l == 0:
            dump([(qT[j], [Qc(j * NG + g) for g in range(NG)], S) for j in range(4)])
        emit_wout(l)
        if not early[1]:
            issue_w2(l, 0)
        issue_w1(l, 1)
        if dbg == "x1" and l == 0:
            dump(xitems())
        if l + 1 < NL:
            for f in ada_setup(l + 1, ADA1):
                bg.append(f)
            emit_ffn(l, lambda g, l=l: norm_m_group(l + 1, g))
        elif dbg:
            emit_ffn(l, lambda g: None)
        else:
            emit_ffn(l, final_group)
        bg_step(len(bg))
    if dbg:
        if dbg == "x":
            dump(xitems())
        for g in range(NG):
            final_group(g)
    sc.emit()
    st.close()
    return nc, sc.stats


def _layout(inputs, b, NL):
    f = lambda a: np.ascontiguousarray(a, dtype=np.float32)
    d = {}
    d["x"] = f(inputs["x"][b])
    d["c"] = f(inputs["c"][b].reshape(8, 128).T)
    d["ada_w"] = f(inputs["ada_w"][:NL])
    d["ada_b"] = f(inputs["ada_b"][:NL].reshape(NL, 48, 128).transpose(2, 0, 1).reshape(128, NL * 48))
    d["gmix"] = f(inputs["norm_mix_g"][:NL].reshape(NL, 8, 128).transpose(2, 0, 1).reshape(128, NL * 8))
    d["gmlp"] = f(inputs["norm_mlp_g"][:NL].reshape(NL, 8, 128).transpose(2, 0, 1).reshape(128, NL * 8))
    d["gfin"] = f(inputs["final_norm_g"].reshape(8, 128).T)
    d["w_in"] = f(inputs["w_in"][:NL])
    d["conv_w"] = f(inputs["conv_w"][:NL].reshape(NL, 3, 2, 128).transpose(3, 0, 2, 1).reshape(128, NL * 6))
    d["conv_b"] = f(inputs["conv_b"][:NL].reshape(NL, 2, 128).transpose(2, 0, 1).reshape(128, NL * 2))
    d["gng"] = f(np.broadcast_to(inputs["gmlp_norm_g"][:NL].reshape(1, NL * 256), (128, NL * 256)))
    d["swT"] = f(inputs["spatial_w"][:NL].transpose(3, 0, 1, 2).reshape(128, NL * 512))
    d["sb"] = f(inputs["spatial_b"][:NL].reshape(1, NL * 512))
    d["w_out"] = f(inputs["w_out"][:NL])
    d["w1"] = f(inputs["mlp_w1"][:NL])
    d["w2"] = f(inputs["mlp_w2"][:NL])
    return d


_NC_CACHE = {}


def kernel(**inputs):
    x = np.asarray(inputs["x"])
    B, S, _ = x.shape
    NL = int(np.asarray(inputs["ada_w"]).shape[0])
    inputs = {k: np.asarray(v) for k, v in inputs.items()}
    key = (S, NL)
    if key not in _NC_CACHE:
        _NC_CACHE[key] = build(S, NL)[0]
    nc = _NC_CACHE[key]
    shared = _layout(inputs, 0, NL)
    in_maps = []
    for b in range(B):
        d = dict(shared)
        d["x"] = np.ascontiguousarray(inputs["x"][b], dtype=np.float32)
        d["c"] = np.ascontiguousarray(inputs["c"][b].reshape(8, 128).T, dtype=np.float32)
        in_maps.append(d)
    res = run_bass_kernel_spmd(nc, in_maps, core_ids=list(range(B)))
    out = np.stack([np.asarray(r["out"]) for r in res.results], axis=0)
    return out.astype(np.float32)
```

```python
import contextlib
from collections import deque
import numpy as np
import concourse.bass as bass
import concourse.mybir as mybir
from concourse.bass_utils import run_bass_kernel_spmd

F32 = mybir.dt.float32
BF16 = mybir.dt.bfloat16
AF = mybir.ActivationFunctionType
ALU = mybir.AluOpType
AX = mybir.AxisListType

ENGS = ("pe", "act", "dve", "pool", "sp")
EPS = 1e-6


def I(method, *args, **kw):
    return lambda e: getattr(e, method)(*args, **kw)


class Op:
    __slots__ = ("eng", "fn", "reads", "writes", "dma", "deps", "signal", "sig", "idx", "eidx", "dsem", "dval",
                 "dsem_i")

    def __init__(self, eng, fn, reads, writes, dma):
        self.eng, self.fn, self.reads, self.writes, self.dma = eng, fn, reads, writes, dma
        self.deps = []
        self.signal = False
        self.sig = None
        self.dsem = None
        self.dval = None


class Sched:
    def __init__(self, nc, n_dma_sems=8, self_dist=10 ** 9):
        self.nc = nc
        self.ops = []
        self.lastw = {}
        self.readers = {}
        self.ecount = {e: 0 for e in ENGS}
        self.n_dma_sems = n_dma_sems
        self.self_dist = self_dist

    def op(self, eng, fn, reads=(), writes=(), dma=False):
        o = Op(eng, fn, tuple(reads), tuple(writes), dma)
        o.idx = len(self.ops)
        o.eidx = self.ecount[eng]
        self.ecount[eng] += 1
        deps = set()
        for t in o.reads:
            w = self.lastw.get(t)
            if w is not None:
                deps.add(w)
        for t in o.writes:
            w = self.lastw.get(t)
            if w is not None:
                deps.add(w)
            rd = self.readers.get(t)
            if rd:
                for v in rd[0].values():
                    deps.add(v)
                for v in rd[1]:
                    deps.add(v)
        deps.discard(o.idx)
        o.deps = sorted(deps)
        for t in o.reads:
            rd = self.readers.get(t)
            if rd is None:
                rd = self.readers[t] = ({}, [])
            if dma:
                rd[1].append(o.idx)
            else:
                rd[0][eng] = o.idx
        for t in o.writes:
            self.lastw[t] = o.idx
            self.readers[t] = ({}, [])
        self.ops.append(o)
        return o

    def pe(self, fn, reads=(), writes=()):
        return self.op("pe", fn, reads, writes)

    def act(self, fn, reads=(), writes=()):
        return self.op("act", fn, reads, writes)

    def dve(self, fn, reads=(), writes=()):
        return self.op("dve", fn, reads, writes)

    def pool(self, fn, reads=(), writes=()):
        return self.op("pool", fn, reads, writes)

    def dma(self, q, fn, reads=(), writes=()):
        return self.op(q, fn, reads, writes, dma=True)

    def _needs_sem(self, a, b):
        if a.dma:
            return True
        if a.eng != b.eng:
            return True
        if b.dma:
            return True
        if a.eng == "pe":
            return False
        return (b.eidx - a.eidx) <= self.self_dist

    def emit(self):
        nc = self.nc
        ops = self.ops
        for b in ops:
            for ai in b.deps:
                a = ops[ai]
                if not a.dma and self._needs_sem(a, b):
                    a.signal = True
        with contextlib.ExitStack() as st:
            csem = {e: st.enter_context(nc.semaphore("c_" + e)) for e in ("pe", "act", "dve", "pool")}
            dsems = {}
            for q in ("sp", "pool"):
                dsems[q] = [st.enter_context(nc.semaphore("d_%s%d" % (q, i))) for i in range(self.n_dma_sems)]
            dcnt = {q: [0] * self.n_dma_sems for q in dsems}
            dnext = {q: 0 for q in dsems}
            ccount = {e: 0 for e in csem}
            waited = {e: {} for e in ENGS}
            streams = {e: [] for e in ENGS}
            for o in ops:
                waits = []
                W = waited[o.eng]

                def need(sem, val, key):
                    if W.get(key, 0) < val:
                        W[key] = val
                        waits.append((sem, val))

                for ai in o.deps:
                    a = ops[ai]
                    if a.dma:
                        need(a.dsem, a.dval, ("d", a.eng, a.dsem_i))
                    elif a.signal and self._needs_sem(a, o):
                        need(csem[a.eng], a.sig, ("c", a.eng))
                if o.dma:
                    q = o.eng
                    i = dnext[q]
                    dnext[q] = (i + 1) % self.n_dma_sems
                    sem = dsems[q][i]
                    if dcnt[q][i] > 0:
                        need(sem, dcnt[q][i], ("d", q, i))
                    dcnt[q][i] += 16
                    o.dsem, o.dval = sem, dcnt[q][i]
                    o.dsem_i = i
                    streams[o.eng].append((waits, o.fn, sem, 16))
                else:
                    if o.signal:
                        ccount[o.eng] += 1
                        o.sig = ccount[o.eng]
                        streams[o.eng].append((waits, o.fn, csem[o.eng], 1))
                    else:
                        streams[o.eng].append((waits, o.fn, None, 0))
            finals = []
            for q in dsems:
                for i, sem in enumerate(dsems[q]):
                    if dcnt[q][i] > 0:
                        finals.append((sem, dcnt[q][i]))
            for e in csem:
                if ccount[e] > 0:
                    finals.append((csem[e], ccount[e]))
            self.stats = {e: len(streams[e]) for e in ENGS}
            self.stats["sig"] = dict(ccount)

            def run(engine, lst, extra=None):
                for waits, fn, sem, inc in lst:
                    for (s, v) in waits:
                        engine.wait_ge(s, v)
                    ins = fn(engine)
                    if sem is not None:
                        ins.then_inc(sem, inc)
                if extra:
                    for (s, v) in extra:
                        engine.wait_ge(s, v)

            with nc.Block() as block:
                @block.tensor
                def _(e):
                    run(e, streams["pe"])

                @block.scalar
                def _(e):
                    run(e, streams["act"])

                @block.vector
                def _(e):
                    run(e, streams["dve"])

                @block.gpsimd
                def _(e):
                    run(e, streams["pool"])

                @block.sync
                def _(e):
                    run(e, streams["sp"], extra=finals)


WIN_ORDER = [10, 9, 7, 8, 6, 4, 5, 2, 3, 0, 1]


def build(S, NL, dbg=None):
    NG = S // 512
    NT = S // 128
    nc = bass.Bass("TRN2", target_bir_lowering=False)

    def din(name, shape):
        return nc.dram_tensor(name, shape, F32, kind="ExternalInput").ap()

    x_d = din("x", [S, 1024])
    c_d = din("c", [128, 8])
    adaw_d = din("ada_w", [NL, 1024, 6144])
    adab_d = din("ada_b", [128, NL * 48])
    gmix_d = din("gmix", [128, NL * 8])
    gmlp_d = din("gmlp", [128, NL * 8])
    gfin_d = din("gfin", [128, 8])
    win_d = din("w_in", [NL, 1024, 2816])
    cw_d = din("conv_w", [128, NL * 6])
    cb_d = din("conv_b", [128, NL * 2])
    gng_d = din("gng", [128, NL * 256])
    swT_d = din("swT", [128, NL * 512])
    sb_d = din("sb", [1, NL * 512])
    wout_d = din("w_out", [NL, 1024, 1024])
    w1_d = din("w1", [NL, 1024, 4096])
    w2_d = din("w2", [NL, 4096, 1024])
    out_d = nc.dram_tensor("out", [S, 1024], F32, kind="ExternalOutput").ap()
    dbg_d = None
    if dbg:
        dbg_d = nc.dram_tensor("dbg", [128, 8 * S], F32, kind="ExternalOutput").ap()

    adaw_v = [adaw_d[l].rearrange("(k p) c -> p k c", p=128) for l in range(NL)]
    win_v = [win_d[l].rearrange("(k p) c -> p k c", p=128) for l in range(NL)]
    wout_v = [wout_d[l].rearrange("(k p) c -> p k c", p=128) for l in range(NL)]
    w1_v = [w1_d[l].rearrange("(k p) c -> p k c", p=128) for l in range(NL)]
    w2_v = [w2_d[l].rearrange("(k p) c -> p k c", p=128) for l in range(NL)]
    x_v = x_d.rearrange("(t p) d -> p t d", p=128)
    out_v = out_d.rearrange("(t p) d -> p t d", p=128)

    Hcells = max(8 * NG, 32)
    Qcells = max(12 * NG, 4 * NG + 32)
    if NG == 1:
        Qcells = 36
    off = [0]

    def carve(nbytes):
        o = off[0]
        off[0] += (nbytes + 63) // 64 * 64
        return o

    o_x = carve(8 * S * 4)
    o_H = carve(Hcells * 1024)
    o_Q = carve(Qcells * 1024)
    o_M = carve(8 * S)
    o_S = carve(16 * 1024)
    o_W = carve(16 * 1024)
    o_identf = carve(512)
    o_ctmp = carve(512)
    o_identb = carve(256)
    o_negtri = carve(256)
    o_negones = carve(256)
    o_ones = carve(256)
    o_negm = carve(256)
    o_c = carve(32)
    o_cact = carve(16)
    o_mod = carve(NL * 48 * 4)
    o_adab = carve(NL * 48 * 4)
    o_gmix = carve(NL * 8 * 4)
    o_gmlp = carve(NL * 8 * 4)
    o_gfin = carve(32)
    o_gsm = carve(NL * 8 * 4)
    o_gsf = carve(NL * 8 * 4)
    o_cw = carve(NL * 6 * 4)
    o_cb = carve(NL * 2 * 4)
    o_gng = carve(1024)
    o_gng16 = carve(1024)
    o_swT = carve(1024)
    o_sbb = carve(1024)
    o_modrow = carve(2048)
    o_vstat = carve(256)
    TOTAL = off[0]
    assert TOTAL <= 206 * 1024, TOTAL

    st = contextlib.ExitStack()
    arena = st.enter_context(nc.sbuf_tensor("arena", [128, TOTAL // 2], BF16))
    ps = st.enter_context(nc.psum_tensor("ps", [128, 8, 512], F32))

    def V(o, shape, dt):
        esz = 4 if dt == F32 else 2
        n = int(np.prod(shape))
        ap = arena[:, o // 2: o // 2 + n * esz // 2]
        if dt == F32:
            ap = ap.bitcast(F32)
        if len(shape) == 2:
            ap = ap.rearrange("p (a b) -> p a b", b=shape[1])
        elif len(shape) == 3:
            ap = ap.rearrange("p (a b c) -> p a b c", b=shape[1], c=shape[2])
        return ap

    Hc = lambda c: ("H", c)
    Qc = lambda c: ("Q", c)
    Sc = lambda c: ("S", c)
    Wc = lambda c: ("W", c)

    xT = V(o_x, [8, S], F32)
    hT = V(o_H, [8, S], BF16)
    wo = V(o_W, [8, 1024], BF16)
    Et = [V(o_H + i * 4096, [2, 512], F32) for i in range(2)]
    LT = [V(o_H + (8 + 2 * i) * 1024, [2, 512], BF16) for i in range(3)]
    WT = [V(o_H + (14 + 2 * i) * 1024, [2, 512], BF16) for i in range(3)]
    Rt = V(o_H + 20 * 1024, [2, 512], BF16)
    qT = [V(o_Q + (j * NG) * 1024, [S], BF16) for j in range(4)]
    kT = [V(o_Q + (4 * NG + j * NG) * 1024, [S], BF16) for j in range(4)]
    vv = V(o_Q + 8 * NG * 1024, [NT, 512], BF16)
    h1 = [V(o_Q + sub * NG * 1024, [S], BF16) for sub in range(4)]
    w1s = [V(o_W, [8, 512], BF16), V(o_Q + 4 * NG * 1024, [8, 512], BF16)]
    w2s = [V(o_W + 8192, [4, 1024], BF16), V(o_Q + (4 * NG + 8) * 1024, [4, 1024], BF16)]
    w1tok = [[Wc(0), Wc(1)], [Qc(4 * NG + i) for i in range(8)]]
    w2tok = [[Wc(2), Wc(3)], [Qc(4 * NG + 8 + i) for i in range(8)]]
    ADA0 = 0
    ADA1 = 4 * NG + 16
    XS0 = 16
    xs = [V(o_Q + (XS0 + i * 16) * 1024, [4, 1024], F32) for i in range(2 if NG > 1 else 1)]
    OF0 = 4 * NG + 16
    OS0 = 4 * NG
    of = V(o_Q + OF0 * 1024, [8, 512], F32)
    ost = [V(o_Q + (OS0 + 4 * i) * 1024, [1024], F32) for i in range(2)]
    cout = [V(o_M + j * S * 2, [S], BF16) for j in range(2)]
    sout = [V(o_M + 2 * S * 2 + j * S * 2, [S], BF16) for j in range(2)]
    gvb = V(o_S, [NT, 256], BF16)
    sqv = V(o_S + 8 * 1024, [256], F32)
    sq = [V(o_S + i * 1024, [512], BF16) for i in range(4)]
    rstds = [V(o_S + 4096 + i * 2048, [512], F32) for i in range(2)]
    tmp = [V(o_S + 8192 + i * 2048, [512], F32) for i in range(4)]
    Cc = V(o_S, [S], BF16)
    uu = V(o_S + 4096, [S + 2], F32)
    ytmp = V(o_S + 13 * 1024, [512], F32)
    wt = [V(o_W + i * 4096, [8, 256], BF16) for i in range(4)]
    ident_f = V(o_identf, [128], F32)
    ctmp = V(o_ctmp, [128], F32)
    ident_b = V(o_identb, [128], BF16)
    negtri = V(o_negtri, [128], BF16)
    negones = V(o_negones, [128], BF16)
    ones_b = V(o_ones, [128], BF16)
    negm = V(o_negm, [128], BF16)
    c_sb = V(o_c, [8], F32)
    cact = V(o_cact, [8], BF16)
    mod = V(o_mod, [NL * 48], F32)
    adab = V(o_adab, [NL * 48], F32)
    gmix = V(o_gmix, [NL * 8], F32)
    gmlp = V(o_gmlp, [NL * 8], F32)
    gfin = V(o_gfin, [8], F32)
    gsm = V(o_gsm, [NL * 8], F32)
    gsf = V(o_gsf, [NL * 8], F32)
    cw = V(o_cw, [NL * 6], F32)
    cb = V(o_cb, [NL * 2], F32)
    gng = V(o_gng, [256], F32)
    gng16 = V(o_gng16, [256], F32)
    swT = V(o_swT, [4, 128], BF16)
    sbb = V(o_sbb, [512], BF16)
    modrow = [V(o_modrow, [512], F32)]
    vstat = V(o_vstat, [64], F32)

    sc = Sched(nc)
    bank_rr = [0]

    def nb():
        b = bank_rr[0]
        bank_rr[0] = (b + 1) % 8
        return b

    bg = deque()

    def bg_step(n=1):
        for _ in range(n):
            if bg:
                bg.popleft()()

    sc.pool(I("memset", ident_f, 1.0), writes=["identf"])
    sc.pool(I("affine_select", out=ident_f, in_=ident_f, pattern=[[-1, 128]], compare_op=ALU.is_equal,
                                      fill=0.0, base=0, channel_multiplier=1), reads=["identf"], writes=["identf"])
    sc.dve(I("tensor_copy", out=ident_b, in_=ident_f), reads=["identf"], writes=["identb"])
    sc.pool(I("memset", ctmp, -1.0), writes=["ctmp"])
    sc.pool(I("affine_select", out=ctmp, in_=ctmp, pattern=[[-1, 128]], compare_op=ALU.is_ge,
                                      fill=0.0, base=0, channel_multiplier=1), reads=["ctmp"], writes=["ctmp"])
    sc.dve(I("tensor_copy", out=negtri, in_=ctmp), reads=["ctmp"], writes=["negtri"])
    sc.pool(I("memset", ctmp, -30000.0), reads=["ctmp"], writes=["ctmp"])
    sc.pool(I("affine_select", out=ctmp, in_=ctmp, pattern=[[-1, 128]], compare_op=ALU.is_ge,
                                      fill=0.0, base=0, channel_multiplier=1), reads=["ctmp"], writes=["ctmp"])
    sc.dve(I("tensor_copy", out=negm, in_=ctmp), reads=["ctmp"], writes=["negm"])
    sc.pool(I("memset", negones, -1.0), writes=["negones"])
    sc.pool(I("memset", ones_b, 1.0), writes=["ones"])
    for (dst, src, tok) in ((c_sb, c_d, "c"), (adab, adab_d, "adab"), (gmix, gmix_d, "gmix"), (gmlp, gmlp_d, "gmlp"),
                            (gfin, gfin_d, "gfin"), (cw, cw_d, "cw"), (cb, cb_d, "cb")):
        sc.dma("sp", I("dma_start", out=dst, in_=src), writes=[tok])
    sc.act(I("activation", out=cact, in_=c_sb, func=AF.Silu), reads=["c"], writes=["cact"])

    def ada_setup(l, base):
        tiles = [V(o_Q + (base + 8 * i) * 1024, [8, 512], BF16) for i in range(2)]
        ttok = [[Qc(base + 8 * i + c) for c in range(8)] for i in range(2)]

        def issue(ct):
            sc.dma("pool", I("dma_start", out=tiles[ct % 2], in_=adaw_v[l][:, :, ct * 512:(ct + 1) * 512]),
                   writes=ttok[ct % 2])

        def item(ct):
            def f():
                if ct + 1 < 12:
                    issue(ct + 1)
                b = nb()
                for k in range(8):
                    sc.pe(I("matmul", ps[0:1, b, :], lhsT=cact[:, k:k + 1], rhs=tiles[ct % 2][:, k, :],
                                                  start=(k == 0), stop=(k == 7)),
                          reads=ttok[ct % 2] + ["cact"], writes=[("ps", b)])
                mr = modrow[0]
                sc.dve(I("tensor_copy", out=mr[0:1, :], in_=ps[0:1, b, :]), reads=[("ps", b)],
                       writes=[("modrow", 0)])
                b2 = nb()
                for jj in range(4):
                    sc.pe(I("matmul", ps[:, b2, jj:jj + 1], lhsT=mr[0:1, jj * 128:(jj + 1) * 128],
                                                    rhs=ident_f[0:1, 0:1], start=True, stop=True),
                          reads=[("modrow", 0), "identf"], writes=[("ps", b2)])
                c0 = l * 48 + ct * 4
                sc.dve(I("tensor_tensor", out=mod[:, c0:c0 + 4], in0=ps[:, b2, 0:4], in1=adab[:, c0:c0 + 4],
                                                 op=ALU.add), reads=[("ps", b2), "adab"], writes=[("mod", l, ct)])
                if ct == 3:
                    sc.dve(I("scalar_tensor_tensor", out=gsm[:, l * 8:l * 8 + 8], in0=mod[:, l * 48 + 8:l * 48 + 16],
                                                            scalar=1.0, in1=gmix[:, l * 8:l * 8 + 8], op0=ALU.add,
                                                            op1=ALU.mult),
                           reads=[("mod", l, 2), ("mod", l, 3), "gmix"], writes=[("gsm", l)])
                if ct == 9:
                    sc.dve(I("scalar_tensor_tensor", out=gsf[:, l * 8:l * 8 + 8], in0=mod[:, l * 48 + 32:l * 48 + 40],
                                                            scalar=1.0, in1=gmlp[:, l * 8:l * 8 + 8], op0=ALU.add,
                                                            op1=ALU.mult),
                           reads=[("mod", l, 8), ("mod", l, 9), "gmlp"], writes=[("gsf", l)])
            return f

        issue(0)
        return [item(ct) for ct in range(12)]

    def modtok(l, c0):
        return [("mod", l, c0 // 4), ("mod", l, c0 // 4 + 1)]

    def emit_norm(gs_ap, sh_ap, gtoks):
        for g in range(NG):
            norm_group(g, gs_ap, sh_ap, gtoks)

    def norm_group(g, gs_ap, sh_ap, gtoks):
        st = norm_s1(g)
        norm_s2(g, st, gs_ap, sh_ap, gtoks)

    def norm_s1(g):
        b = nb()
        gsl = slice(g * 512, (g + 1) * 512)
        for k in range(8):
            sc.act(I("activation", out=sq[k % 4], in_=xT[:, k, gsl], func=AF.Square),
                   reads=[("x", k, g)], writes=[Sc(k % 4)])
            sc.pe(I("matmul", ps[:, b, :], lhsT=ones_b, rhs=sq[k % 4], start=(k == 0), stop=(k == 7)),
                  reads=[Sc(k % 4), "ones"], writes=[("ps", b)])
        return b

    def norm_s2(g, b, gs_ap, sh_ap, gtoks):
        gsl = slice(g * 512, (g + 1) * 512)
        rstd = rstds[g % 2]
        rtk = [Sc(4 + 2 * (g % 2)), Sc(5 + 2 * (g % 2))]
        sc.act(I("activation", out=rstd, in_=ps[:, b, :], func=AF.Ln, scale=1.0 / 1024, bias=EPS),
               reads=[("ps", b)], writes=rtk)
        sc.act(I("activation", out=rstd, in_=rstd, func=AF.Exp, scale=-0.5), reads=rtk, writes=rtk)
        for k in range(8):
            tt = tmp[k % 4]
            ttk = [Sc(8 + 2 * (k % 4)), Sc(9 + 2 * (k % 4))]
            sc.dve(I("scalar_tensor_tensor", out=tt, in0=xT[:, k, gsl], scalar=gs_ap[:, k:k + 1], in1=rstd,
                     op0=ALU.mult, op1=ALU.mult), reads=[("x", k, g)] + rtk + gtoks, writes=ttk)
            if k % 2 == 0:
                sc.act(I("activation", out=hT[:, k, gsl], in_=tt, func=AF.Identity, bias=sh_ap[:, k:k + 1], scale=1.0),
                       reads=ttk + gtoks, writes=[Hc(k * NG + g)])
            else:
                sc.dve(I("tensor_scalar", out=hT[:, k, gsl], in0=tt, scalar1=sh_ap[:, k:k + 1], scalar2=None,
                         op0=ALU.add), reads=ttk + gtoks, writes=[Hc(k * NG + g)])

    def norm_pipe(groups, gs_ap, sh_ap, gtoks):
        prev = None
        for g in groups:
            st = norm_s1(g)
            if prev is not None:
                norm_s2(prev[0], prev[1], gs_ap, sh_ap, gtoks)
            prev = (g, st)
        if prev is not None:
            norm_s2(prev[0], prev[1], gs_ap, sh_ap, gtoks)

    def win_issue(l, n):
        t = WIN_ORDER[n]
        slot = n % 4
        sc.dma("pool", I("dma_start", out=wt[slot], in_=win_v[l][:, :, t * 256:(t + 1) * 256]),
               writes=[Wc(slot)])

    def fm_group(slot, sub, g, evac):
        b = nb()
        for k in range(8):
            sc.pe(I("matmul", ps[:, b, :], lhsT=wt[slot][:, k, sub * 128:(sub + 1) * 128],
                                          rhs=hT[:, k, g * 512:(g + 1) * 512], start=(k == 0), stop=(k == 7)),
                  reads=[Wc(slot), Hc(k * NG + g)], writes=[("ps", b)])
        evac(b)

    def tm_group(slot, tt, evac):
        b = nb()
        for k in range(8):
            sc.pe(I("matmul", ps[:, b, 0:256], lhsT=hT[:, k, tt * 128:(tt + 1) * 128],
                                          rhs=wt[slot][:, k, :], start=(k == 0), stop=(k == 7)),
                  reads=[Wc(slot), Hc(k * NG + tt // 4)], writes=[("ps", b)])
        evac(b)

    def emit_proj(l, norm_hook=None):
        sc.dma("sp", I("dma_start", out=gng, in_=gng_d[:, l * 256:(l + 1) * 256]), writes=["gng"])
        sc.dma("pool", I("dma_start", out=swT, in_=swT_d[:, l * 512:(l + 1) * 512].rearrange("p (a b) -> p a b", b=128)),
               writes=["swT"])
        sc.pool(I("memset", swT[64:128, :, 0:64], 0.0), reads=["swT"], writes=["swT"])
        sc.dma("pool", I("dma_start", out=sbb[0:1, :], in_=sb_d[0:1, l * 512:(l + 1) * 512]), writes=["sbb"])
        for n in range(11):
            t = WIN_ORDER[n]
            slot = n % 4
            if t == 9:
                for sub in range(2):
                    for g in range(NG):
                        def ev(b, sub=sub, g=g):
                            sc.act(I("activation", out=sout[sub][:, g * 512:(g + 1) * 512], in_=ps[:, b, :],
                                     func=AF.Gelu_apprx_tanh), reads=[("ps", b)], writes=[("so", sub, g)])
                        fm_group(slot, sub, g, ev)
                vst = [("vs", tt) for tt in range(NT)]
                sc.act(I("activation", out=vstat[:, NT:2 * NT], in_=vstat[:, 0:NT], func=AF.Ln, scale=1.0 / 256, bias=EPS),
                       reads=vst, writes=["vs2"])
                sc.act(I("activation", out=vstat[:, NT:2 * NT], in_=vstat[:, NT:2 * NT], func=AF.Exp, scale=-0.5),
                       reads=["vs2"], writes=["vs2"])
                def scale_tile(tt):
                    sc.dve(I("scalar_tensor_tensor", out=gvb[:, tt, :], in0=gvb[:, tt, :], scalar=vstat[:, NT + tt:NT + tt + 1],
                             in1=gng16, op0=ALU.mult, op1=ALU.mult),
                           reads=[("gvb", tt), Sc(tt // 2), "vs2", "gng16"], writes=[("gvb", tt)])

                def mm_tile(tt):
                    b2 = nb()
                    for hh in range(4):
                        j = hh // 2
                        sc.pe(I("matmul", ps[:, b2, hh * 128:(hh + 1) * 128], lhsT=gvb[:, tt, j * 128:(j + 1) * 128],
                                rhs=swT[:, hh, :], start=True, stop=False),
                              reads=[("gvb", tt), Sc(tt // 2), "swT"], writes=[("ps", b2)])
                        sc.pe(I("matmul", ps[:, b2, hh * 128:(hh + 1) * 128], lhsT=ones_b[0:1, :],
                                rhs=sbb[0:1, hh * 128:(hh + 1) * 128], start=False, stop=True),
                              reads=["sbb", "ones"], writes=[("ps", b2)])
                    return b2

                def evac_tile(tt, b2):
                    g = tt // 4
                    tsl = slice(tt * 128, (tt + 1) * 128)
                    for hh in range(4):
                        j = hh // 2
                        rb = 64 * (hh % 2)
                        sc.dve(I("tensor_tensor", out=sout[j][rb:rb + 64, tsl], in0=sout[j][rb:rb + 64, tsl],
                                 in1=ps[rb:rb + 64, b2, hh * 128:(hh + 1) * 128], op=ALU.mult),
                               reads=[("ps", b2), ("so", j, g)], writes=[("so", j, g)])

                scale_tile(0)
                if NT > 1:
                    scale_tile(1)
                pend = None
                for tt in range(NT):
                    b2 = mm_tile(tt)
                    if tt + 2 < NT:
                        scale_tile(tt + 2)
                    if pend is not None:
                        evac_tile(*pend)
                    pend = (tt, b2)
                evac_tile(*pend)
            elif t == 10:
                sc.dve(I("tensor_scalar", out=gng16, in0=gng, scalar1=1.0, scalar2=None, op0=ALU.mult),
                       reads=["gng"], writes=["gng16"])
                for tt in range(NT):
                    if norm_hook is not None:
                        norm_hook(tt)

                    def evA(b, tt=tt):
                        sc.act(I("activation", out=gvb[:, tt, :], in_=ps[:, b, 0:256], func=AF.Gelu_apprx_tanh),
                               reads=[("ps", b)], writes=[Sc(tt // 2), ("gvb", tt)])
                        sc.dve(I("tensor_tensor", out=sqv, in0=gvb[:, tt, :], in1=gvb[:, tt, :], op=ALU.mult),
                               reads=[("gvb", tt), Sc(tt // 2)], writes=[Sc(8)])
                        sc.dve(I("reduce_sum", out=vstat[:, tt:tt + 1], in_=sqv, axis=AX.X),
                               reads=[Sc(8)], writes=[("vs", tt)])
                    tm_group(slot, tt, evA)
            elif t == 7:
                pass
            elif t == 8:
                pass
            elif t == 6:
                sC, sH, sB = (n - 2) % 4, (n - 1) % 4, n % 4
                for j in range(2):
                    for g in range(NG):
                        def evC(b, g=g):
                            sc.act(I("activation", out=Cc[:, g * 512:(g + 1) * 512], in_=ps[:, b, :], func=AF.Copy),
                                   reads=[("ps", b)], writes=[Sc(g)])
                        fm_group(sC, j, g, evC)
                    sc.dve(I("memset", uu[:, 0:2], 0.0), writes=[Sc(4)])
                    for g in range(NG):
                        def evH(b, g=g):
                            utok = [Sc(4 + 2 * g + i) for i in range(3)]
                            sc.dve(I("tensor_tensor", out=uu[:, 2 + g * 512:2 + (g + 1) * 512], in0=ps[:, b, :],
                                                             in1=Cc[:, g * 512:(g + 1) * 512], op=ALU.mult),
                                   reads=[("ps", b), Sc(g)], writes=utok)
                        fm_group(sH, j, g, evH)
                    for g in range(NG):
                        def evB(b, g=g, j=j):
                            utok = [Sc(4 + 2 * g + i) for i in range(-2, 3) if 4 + 2 * g + i >= 4]
                            c6 = l * 6 + j * 3
                            ytok = [Sc(13), Sc(14)]
                            sc.dve(I("tensor_scalar", out=ytmp, in0=uu[:, 2 + g * 512:2 + (g + 1) * 512],
                                                             scalar1=cw[:, c6 + 2:c6 + 3], scalar2=None, op0=ALU.mult),
                                   reads=utok + ["cw"], writes=ytok)
                            sc.dve(I("scalar_tensor_tensor", out=ytmp, in0=uu[:, 1 + g * 512:1 + (g + 1) * 512],
                                                                    scalar=cw[:, c6 + 1:c6 + 2], in1=ytmp, op0=ALU.mult,
                                                                    op1=ALU.add),
                                   reads=utok + ytok, writes=ytok)
                            sc.dve(I("scalar_tensor_tensor", out=ytmp, in0=uu[:, g * 512:(g + 1) * 512],
                                                                    scalar=cw[:, c6:c6 + 1], in1=ytmp, op0=ALU.mult,
                                                                    op1=ALU.add),
                                   reads=utok + ytok, writes=ytok)
                            sc.dve(I("scalar_tensor_tensor", out=cout[j][:, g * 512:(g + 1) * 512], in0=ytmp,
                                                                    scalar=cb[:, l * 2 + j:l * 2 + j + 1], in1=ps[:, b, :],
                                                                    op0=ALU.add, op1=ALU.mult),
                                   reads=ytok + [("ps", b), "cb"], writes=[("co", j, g)])
                        fm_group(sB, j, g, evB)
            elif t in (4, 5):
                half = t - 4
                for tt in range(NT):
                    def ev(b, tt=tt):
                        sc.dve(I("tensor_copy", out=vv[:, tt, half * 256:(half + 1) * 256], in_=ps[:, b, 0:256]),
                               reads=[("ps", b)], writes=[Qc(8 * NG + tt)])
                    tm_group(slot, tt, ev)
            elif t in (2, 3):
                for sub in range(2):
                    j = (t - 2) * 2 + sub
                    for g in range(NG):
                        def ev(b, j=j, g=g):
                            sc.act(I("activation", out=kT[j][:, g * 512:(g + 1) * 512], in_=ps[:, b, :], func=AF.Copy),
                                   reads=[("ps", b)], writes=[Qc(4 * NG + j * NG + g)])
                        fm_group(slot, sub, g, ev)
            else:
                if t == 0 and n == 9:
                    pass
                for sub in range(2):
                    j = t * 2 + sub
                    for g in range(NG):
                        def ev(b, j=j, g=g):
                            sc.dve(I("tensor_scalar", out=qT[j][:, g * 512:(g + 1) * 512], in0=ps[:, b, :],
                                                             scalar1=0.125, scalar2=None, op0=ALU.mult),
                                   reads=[("ps", b)], writes=[Qc(j * NG + g)])
                        fm_group(slot, sub, g, ev)
            if t in (7, 8):
                continue
            if t == 6:
                for nn in (n - 2, n - 1, n):
                    if nn + 4 < 11:
                        win_issue(l, nn + 4)
            elif n + 4 < 11:
                win_issue(l, n + 4)
            if n == 8:
                bg_step(len(bg))
            else:
                bg_step(2 if n == 0 else 1)

    def emit_attn(l):
        items = []
        for j in range(4):
            for m in range(NG):
                for c in range(4 * m + 3, -1, -1):
                    items.append((j, m, c))
        n = len(items)

        def geom(it):
            j, m, c = it
            r = c - 4 * m
            diag = r >= 0
            v0 = 128 * r if diag else 0
            return j, m, c, diag, v0

        def ztok(i):
            return [("ps", 2 * (i % 3)), ("ps", 2 * (i % 3) + 1)]

        def stageA(i):
            j, m, c, diag, v0 = geom(items[i])
            t0 = m * 512
            ktok = Qc(4 * NG + j * NG + c // 4)
            qtok = Qc(j * NG + m)
            for hp in range(2):
                rb = 64 * hp
                zb = 2 * (i % 3) + hp
                kk = kT[j][rb:rb + 64, c * 128:(c + 1) * 128]
                if diag:
                    rest = v0 + 128 < 512
                    if rest:
                        sc.pe(I("matmul", ps[:, zb, v0 + 128:512], lhsT=kk, rhs=qT[j][rb:rb + 64, t0 + v0 + 128:t0 + 512],
                                start=True, stop=False, skip_group_check=True), reads=[ktok, qtok], writes=[("ps", zb)])
                    sc.pe(I("matmul", ps[:, zb, v0:v0 + 128], lhsT=kk, rhs=qT[j][rb:rb + 64, t0 + v0:t0 + v0 + 128],
                            start=not rest, stop=False, skip_group_check=True), reads=[ktok, qtok], writes=[("ps", zb)])
                    sc.pe(I("matmul", ps[:, zb, v0:v0 + 128], lhsT=ident_b, rhs=negm, start=False, stop=True,
                            skip_group_check=True), reads=["identb", "negm"], writes=[("ps", zb)])
                else:
                    sc.pe(I("matmul", ps[:, zb, :], lhsT=kk, rhs=qT[j][rb:rb + 64, t0:t0 + 512], start=True, stop=True,
                            skip_group_check=True), reads=[ktok, qtok], writes=[("ps", zb)])

        def stageB(i):
            j, m, c, diag, v0 = geom(items[i])
            z0 = 2 * (i % 3)
            E = Et[i % 2]
            etok = [Hc(4 * (i % 2) + q) for q in range(4)]
            L = LT[i % 3]
            ltok = [Hc(8 + 2 * (i % 3)), Hc(9 + 2 * (i % 3))]
            sc.act(I("activation", out=E[:, :, v0:512], in_=ps[:, z0:z0 + 2, v0:512], func=AF.Exp),
                   reads=ztok(i), writes=etok)
            sc.act(I("activation", out=L[:, :, v0:512], in_=E[:, :, v0:512], func=AF.Ln, bias=1.0),
                   reads=etok, writes=ltok)

        def stageC(i):
            j, m, c, diag, v0 = geom(items[i])
            z0 = 2 * (i % 3)
            L = LT[i % 3]
            ltok = [Hc(8 + 2 * (i % 3)), Hc(9 + 2 * (i % 3))]
            wtok = [Hc(14 + 2 * (i % 3)), Hc(15 + 2 * (i % 3))]
            rtok = [Hc(20), Hc(21)]
            cmax = 4 * m + 3
            r0 = v0 + 128 if diag else 0
            hasR = (c < cmax) and r0 < 512
            for hp in range(2):
                zb = z0 + hp
                sc.pe(I("matmul", ps[:, zb, v0:512], lhsT=negtri, rhs=L[:, hp, v0:512], start=False, stop=not hasR,
                        skip_group_check=True), reads=ltok + ["negtri"], writes=[("ps", zb)])
                if hasR:
                    sc.pe(I("matmul", ps[:, zb, r0:512], lhsT=negones, rhs=Rt[:, hp, r0:512], start=False, stop=True,
                            skip_group_check=True), reads=rtok + ["negones"], writes=[("ps", zb)])
            sc.act(I("activation", out=WT[i % 3][:, :, v0:512], in_=ps[:, z0:z0 + 2, v0:512], func=AF.Exp),
                   reads=ztok(i), writes=wtok)
            if c > 0:
                if c == cmax:
                    sc.pool(I("memset", Rt, 0.0), writes=rtok)
                sc.pool(I("tensor_tensor", out=Rt[:, :, v0:512], in0=Rt[:, :, v0:512], in1=L[:, :, v0:512], op=ALU.add),
                        reads=rtok + ltok, writes=rtok)

        def stageE(i):
            j, m, c, diag, v0 = geom(items[i])
            W = WT[i % 3]
            wtok = [Hc(14 + 2 * (i % 3)), Hc(15 + 2 * (i % 3))]
            vtok = Qc(8 * NG + c)
            last = (c == 0)
            first = (c == 4 * m + 3)
            for hp in range(2):
                sc.pe(I("matmul", ps[64 * hp:64 * hp + 64, 6, v0:512], lhsT=vv[:, c, (2 * j + hp) * 64:(2 * j + hp + 1) * 64],
                        rhs=W[:, hp, v0:512], start=first, stop=last, skip_group_check=True),
                      reads=[vtok] + wtok, writes=[("ps", 6)])
            if last:
                sc.dve(I("tensor_copy", out=qT[j][:, m * 512:(m + 1) * 512], in_=ps[:, 6, :]),
                       reads=[("ps", 6)], writes=[Qc(j * NG + m)])
                for ct in range(8):
                    fillers.append((l, [j], m, ct))

        for g in range(NG):
            for ct in range(8):
                fillers.append((l, [4, 5], g, ct))
                fillers.append((l, [6, 7], g, ct))

        popped = [0]
        early = [False, False]
        for i in range(n + 2):
            if i < n:
                stageA(i)
                stageB(i)
            if i >= 4:
                pop_filler(1)
                popped[0] += 1
                if popped[0] == 16 * NG + 2 and not early[1]:
                    issue_w2(l, 0)
                    early[1] = True
            if NG >= 4 and i < n and items[i][0] == 2 and items[i][1] == 0 and items[i][2] == 3:
                issue_w1(l, 0)
                early[0] = True
            if 0 <= i - 1 < n:
                stageC(i - 1)
            if 0 <= i - 2 < n:
                stageE(i - 2)
        return early

    fillers = deque()

    def wout_partial(l, kts, g, ct, bank=7):
        mix = [qT[0], qT[1], qT[2], qT[3], cout[0], cout[1], sout[0], sout[1]]

        def mtok(kt):
            if kt < 4:
                return Qc(kt * NG + g)
            if kt < 6:
                return ("co", kt - 4, g)
            return ("so", kt - 6, g)
        for q, kt in enumerate(kts):
            sc.pe(I("matmul", ps[:, bank, :], lhsT=wo[:, kt, ct * 128:(ct + 1) * 128], rhs=mix[kt][:, g * 512:(g + 1) * 512],
                    start=(q == 0), stop=(q == len(kts) - 1)),
                  reads=[Wc(kt // 2), mtok(kt)], writes=[("ps", bank)])
        sc.dve(I("scalar_tensor_tensor", out=xT[:, ct, g * 512:(g + 1) * 512], in0=ps[:, bank, :],
                 scalar=mod[:, l * 48 + 16 + ct:l * 48 + 17 + ct], in1=xT[:, ct, g * 512:(g + 1) * 512],
                 op0=ALU.mult, op1=ALU.add),
               reads=[("ps", bank), ("x", ct, g)] + modtok(l, 16), writes=[("x", ct, g)])

    def pop_filler(k=1):
        for _ in range(k):
            if fillers:
                wout_partial(*fillers.popleft())

    def emit_wout(l):
        args = (gsf[:, l * 8:(l + 1) * 8], mod[:, l * 48 + 24:l * 48 + 32], [("gsf", l)] + modtok(l, 24))
        pending = sorted(set(f[2] for f in fillers))
        norm_pipe([g for g in range(NG) if g not in pending], *args)
        while fillers:
            wout_partial(*fillers.popleft(), bank=nb())
        norm_pipe(pending, *args)

    def issue_w1(l, ft):
        if ft < 8:
            s1 = (ft + 1) % 2
            sc.dma("pool", I("dma_start", out=w1s[s1], in_=w1_v[l][:, :, ft * 512:(ft + 1) * 512]), writes=w1tok[s1])

    def issue_w2(l, ft):
        if ft < 8:
            s2 = ft % 2
            sc.dma("pool", I("dma_start", out=w2s[s2], in_=w2_v[l][:, ft * 4:(ft + 1) * 4, :]), writes=w2tok[s2])

    def emit_ffn(l, tail):
        for ft in range(8):
            s1 = (ft + 1) % 2
            s2 = ft % 2
            issue_w2(l, ft + 1)
            order1 = [(sub, g) for sub in range(4) for g in range(NG)]
            if ft == 0:
                order1 = [(sub, g) for g in range(NG) for sub in range(4)]
            for (sub, g) in order1:
                b = nb()
                for k in range(8):
                    sc.pe(I("matmul", ps[:, b, :], lhsT=w1s[s1][:, k, sub * 128:(sub + 1) * 128],
                            rhs=hT[:, k, g * 512:(g + 1) * 512], start=(k == 0), stop=(k == 7)),
                          reads=w1tok[s1] + [Hc(k * NG + g)], writes=[("ps", b)])
                tq = tmp[b % 4]
                tqk = [Sc(8 + 2 * (b % 4)), Sc(9 + 2 * (b % 4))]
                sc.act(I("activation", out=tq, in_=ps[:, b, :], func=AF.Square), reads=[("ps", b)], writes=tqk)
                sc.dve(I("scalar_tensor_tensor", out=h1[sub][:, g * 512:(g + 1) * 512], in0=ps[:, b, :], scalar=0.0,
                         in1=tq, op0=ALU.is_gt, op1=ALU.mult), reads=[("ps", b)] + tqk, writes=[Qc(sub * NG + g)])
            bg_step(1)
            issue_w1(l, ft + 2)
            if ft == 7 and l + 1 < NL:
                for n in range(4):
                    win_issue(l + 1, n)
            order2 = [(ct, g) for ct in range(8) for g in range(NG)]
            if ft == 7:
                order2 = [(ct, g) for g in range(NG) for ct in range(8)]
            for (ct, g) in order2:
                b = nb()
                for kk in range(4):
                    sc.pe(I("matmul", ps[:, b, :], lhsT=w2s[s2][:, kk, ct * 128:(ct + 1) * 128],
                            rhs=h1[kk][:, g * 512:(g + 1) * 512], start=(kk == 0), stop=(kk == 3)),
                          reads=w2tok[s2] + [Qc(kk * NG + g)], writes=[("ps", b)])
                sc.dve(I("scalar_tensor_tensor", out=xT[:, ct, g * 512:(g + 1) * 512], in0=ps[:, b, :],
                         scalar=mod[:, l * 48 + 40 + ct:l * 48 + 41 + ct], in1=xT[:, ct, g * 512:(g + 1) * 512],
                         op0=ALU.mult, op1=ALU.add),
                       reads=[("ps", b), ("x", ct, g)] + modtok(l, 40), writes=[("x", ct, g)])
                if ft == 7 and ct == 7 and g >= 1:
                    tail(g - 1)
            if ft == 7:
                tail(NG - 1)
            bg_step(1)

    def final_group(g):
        if True:
            b = nb()
            gsl = slice(g * 512, (g + 1) * 512)
            rstd = rstds[g % 2]
            rtk = [Sc(4 + 2 * (g % 2)), Sc(5 + 2 * (g % 2))]
            for k in range(8):
                sc.act(I("activation", out=sq[k % 4], in_=xT[:, k, gsl], func=AF.Square),
                       reads=[("x", k, g)], writes=[Sc(k % 4)])
                sc.pe(I("matmul", ps[:, b, :], lhsT=ones_b, rhs=sq[k % 4], start=(k == 0), stop=(k == 7)),
                      reads=[Sc(k % 4), "ones"], writes=[("ps", b)])
            sc.act(I("activation", out=rstd, in_=ps[:, b, :], func=AF.Ln, scale=1.0 / 1024, bias=EPS),
                   reads=[("ps", b)], writes=rtk)
            sc.act(I("activation", out=rstd, in_=rstd, func=AF.Exp, scale=-0.5), reads=rtk, writes=rtk)
            for k in range(8):
                sc.dve(I("scalar_tensor_tensor", out=of[:, k, :], in0=xT[:, k, gsl], scalar=gfin[:, k:k + 1],
                         in1=rstd, op0=ALU.mult, op1=ALU.mult),
                       reads=[("x", k, g), "gfin"] + rtk, writes=[Qc(OF0 + 2 * k), Qc(OF0 + 2 * k + 1)])
            for i in range(4):
                tt = g * 4 + i
                oi = tt % 2
                otok = [Qc(OS0 + 4 * oi + c) for c in range(4)]
                for half in range(2):
                    b2 = nb()
                    for kq in range(4):
                        k = half * 4 + kq
                        sc.pe(I("transpose", ps[:, b2, kq * 128:(kq + 1) * 128],
                                                                of[:, k, i * 128:(i + 1) * 128], ident_f),
                              reads=[Qc(OF0 + 2 * k), Qc(OF0 + 2 * k + 1), "identf"], writes=[("ps", b2)])
                    if half == 0:
                        sc.act(I("activation", out=ost[oi][:, 0:512], in_=ps[:, b2, :], func=AF.Copy),
                               reads=[("ps", b2)], writes=otok)
                    else:
                        sc.dve(I("tensor_copy", out=ost[oi][:, 512:1024], in_=ps[:, b2, :]),
                               reads=[("ps", b2)], writes=otok)
                sc.dma("sp", I("dma_start", out=out_v[:, tt, :], in_=ost[oi]), reads=otok)

    def dump(items):
        o = 0
        for ap, toks, n in items:
            sc.dma("pool", I("dma_start", out=dbg_d[:, o:o + n], in_=ap), reads=toks)
            o += n

    def xitems():
        return [(xT[:, k, :], [("x", k, g) for g in range(NG)], S) for k in range(8)]

    ada0 = ada_setup(0, ADA0)
    n_up = 4 if NG >= 4 else 12

    def norm_m_group(l, g):
        norm_group(g, gsm[:, l * 8:(l + 1) * 8], mod[:, l * 48:l * 48 + 8], [("gsm", l)] + modtok(l, 0))

    def xload(g):
        xi = g % len(xs)
        xtok = [Qc(XS0 + xi * 16 + c) for c in range(16)]
        sc.dma("sp", I("dma_start", out=xs[xi], in_=x_v[:, g * 4:(g + 1) * 4, :]), writes=xtok)
        for k in range(8):
            b = nb()
            for i in range(4):
                sc.pe(I("transpose", ps[:, b, i * 128:(i + 1) * 128], xs[xi][:, i, k * 128:(k + 1) * 128], ident_f),
                      reads=xtok + ["identf"], writes=[("ps", b)])
            if k % 2 == 0:
                sc.act(I("activation", out=xT[:, k, g * 512:(g + 1) * 512], in_=ps[:, b, :], func=AF.Copy),
                       reads=[("ps", b)], writes=[("x", k, g)])
            else:
                sc.dve(I("tensor_copy", out=xT[:, k, g * 512:(g + 1) * 512], in_=ps[:, b, :]),
                       reads=[("ps", b)], writes=[("x", k, g)])

    up = list(ada0[:n_up])
    half = (len(up) + 1) // 2
    for g in range(NG):
        xload(g)
        if g == 0:
            for f in up[:half]:
                f()
        if g == min(1, NG - 1):
            for f in up[half:] if NG > 1 else up[half:]:
                f()
        if g >= 2:
            norm_m_group(0, g - 2)
    for f in ada0[n_up:]:
        bg.append(f)
    for n in range(4):
        win_issue(0, n)
    norm_pipe(list(range(max(NG - 2, 0), NG)), gsm[:, 0:8], mod[:, 0:8], [("gsm", 0)] + modtok(0, 0))
    l0_hook = None
    if dbg == "x0":
        dump(xitems())
    for l in range(NL):
        if dbg == "h0" and l == 0:
            dump([(hT[:, k, :], [Hc(k * NG + g) for g in range(NG)], S) for k in range(8)])
        emit_proj(l, None)
        bg_step(len(bg))
        if dbg == "qk" and l == 0:
            dump([(qT[j], [Qc(j * NG + g) for g in range(NG)], S) for j in range(4)] +
                 [(kT[j], [Qc(4 * NG + j * NG + g) for g in range(NG)], S) for j in range(4)])
        if dbg == "v" and l == 0:
            dump([(vv[:, tt, :], [Qc(8 * NG + tt)], 512) for tt in range(NT)])
        if dbg == "cs" and l == 0:
            dump([(cout[j], [("co", j, g) for g in range(NG)], S) for j in range(2)] +
                 [(sout[j], [("so", j, g) for g in range(NG)], S) for j in range(2)])
        for q4 in range(4):
            sc.dma("pool", I("dma_start", out=wo[:, 2 * q4:2 * q4 + 2, :], in_=wout_v[l][:, 2 * q4:2 * q4 + 2, :]),
                   writes=[Wc(q4)])
        early = emit_attn(l)
        if not early[0]:
            issue_w1(l, 0)
        if dbg == "att" and l == 0:
            dump([(qT[j], [Qc(j * NG + g) for g in range(NG)], S) for j in range(4)])
        emit_wout(l)
        if not early[1]:
            issue_w2(l, 0)
        issue_w1(l, 1)
        if dbg == "x1" and l == 0:
            dump(xitems())
        if l + 1 < NL:
            for f in ada_setup(l + 1, ADA1):
                bg.append(f)
            emit_ffn(l, lambda g, l=l: norm_m_group(l + 1, g))
        elif dbg:
            emit_ffn(l, lambda g: None)
        else:
            emit_ffn(l, final_group)
        bg_step(len(bg))
    if dbg:
        if dbg == "x":
            dump(xitems())
        for g in range(NG):
            final_group(g)
    sc.emit()
    st.close()
    return nc, sc.stats


def _layout(inputs, b, NL):
    f = lambda a: np.ascontiguousarray(a, dtype=np.float32)
    d = {}
    d["x"] = f(inputs["x"][b])
    d["c"] = f(inputs["c"][b].reshape(8, 128).T)
    d["ada_w"] = f(inputs["ada_w"][:NL])
    d["ada_b"] = f(inputs["ada_b"][:NL].reshape(NL, 48, 128).transpose(2, 0, 1).reshape(128, NL * 48))
    d["gmix"] = f(inputs["norm_mix_g"][:NL].reshape(NL, 8, 128).transpose(2, 0, 1).reshape(128, NL * 8))
    d["gmlp"] = f(inputs["norm_mlp_g"][:NL].reshape(NL, 8, 128).transpose(2, 0, 1).reshape(128, NL * 8))
    d["gfin"] = f(inputs["final_norm_g"].reshape(8, 128).T)
    d["w_in"] = f(inputs["w_in"][:NL])
    d["conv_w"] = f(inputs["conv_w"][:NL].reshape(NL, 3, 2, 128).transpose(3, 0, 2, 1).reshape(128, NL * 6))
    d["conv_b"] = f(inputs["conv_b"][:NL].reshape(NL, 2, 128).transpose(2, 0, 1).reshape(128, NL * 2))
    d["gng"] = f(np.broadcast_to(inputs["gmlp_norm_g"][:NL].reshape(1, NL * 256), (128, NL * 256)))
    d["swT"] = f(inputs["spatial_w"][:NL].transpose(3, 0, 1, 2).reshape(128, NL * 512))
    d["sb"] = f(inputs["spatial_b"][:NL].reshape(1, NL * 512))
    d["w_out"] = f(inputs["w_out"][:NL])
    d["w1"] = f(inputs["mlp_w1"][:NL])
    d["w2"] = f(inputs["mlp_w2"][:NL])
    return d


_NC_CACHE = {}


def kernel(**inputs):
    x = np.asarray(inputs["x"])
    B, S, _ = x.shape
    NL = int(np.asarray(inputs["ada_w"]).shape[0])
    inputs = {k: np.asarray(v) for k, v in inputs.items()}
    key = (S, NL)
    if key not in _NC_CACHE:
        _NC_CACHE[key] = build(S, NL)[0]
    nc = _NC_CACHE[key]
    shared = _layout(inputs, 0, NL)
    in_maps = []
    for b in range(B):
        d = dict(shared)
        d["x"] = np.ascontiguousarray(inputs["x"][b], dtype=np.float32)
        d["c"] = np.ascontiguousarray(inputs["c"][b].reshape(8, 128).T, dtype=np.float32)
        in_maps.append(d)
    res = run_bass_kernel_spmd(nc, in_maps, core_ids=list(range(B)))
    out = np.stack([np.asarray(r["out"]) for r in res.results], axis=0)
    return out.astype(np.float32)
```
